# Optimizing a Trainium2 kernel written in Bass

```python
import math
import jax, jax.numpy as jnp
from jax import lax
import numpy as np

D_MODEL = 4096
BATCH = 2
SEQ = 4096
DEPTH = 4

PLE_DIM = 256
BLOCK = 128
EPS = 1e-6
FOX_HEADS = 16
FOX_HEAD_DIM = 128
FOX_WIDTH = FOX_HEADS * FOX_HEAD_DIM
SWA_Q_HEADS = 32
SWA_KV_HEADS = 4
SWA_HEAD_DIM = 64
SWA_WIDTH = SWA_Q_HEADS * SWA_HEAD_DIM
SWA_KV_WIDTH = SWA_KV_HEADS * SWA_HEAD_DIM
SWA_WINDOW = 128
DSA_HEADS = 32
DSA_QK_DIM = 128
DSA_V_DIM = 128
DSA_WIDTH = DSA_HEADS * DSA_V_DIM
DSA_Q_LATENT = 1024
DSA_KV_LATENT = 512
IDX_HEADS = 32
IDX_DIM = 64
TOPK_MAX = 256
T5_BUCKETS = 32
T5_MAX_DIST = 128
T5_COLS = SWA_Q_HEADS + DSA_HEADS

EVEN_SIZES = (FOX_WIDTH, FOX_WIDTH, FOX_WIDTH, FOX_HEADS, FOX_WIDTH,
              SWA_WIDTH, SWA_KV_WIDTH, SWA_KV_WIDTH, SWA_WIDTH)
EVEN_IN = 4 * FOX_WIDTH + FOX_HEADS + 2 * SWA_WIDTH + 2 * SWA_KV_WIDTH
EVEN_OUT = FOX_WIDTH + SWA_WIDTH
ODD_SIZES = (DSA_Q_LATENT, DSA_KV_LATENT, IDX_DIM, IDX_HEADS, DSA_WIDTH)
ODD_IN = DSA_Q_LATENT + DSA_KV_LATENT + IDX_DIM + IDX_HEADS + DSA_WIDTH
N_EVEN = (DEPTH + 1) // 2
N_ODD = DEPTH // 2

kernel_name = 'hybrid_fox_swa_dsa_block'

F32 = jnp.float32


def _split(t, sizes):
    offs = [int(v) for v in np.cumsum(sizes)[:-1]]
    return jnp.split(t, offs, axis=-1)


def rms_norm(x, g):
    xf = x.astype(F32)
    y = xf * lax.rsqrt(jnp.mean(xf * xf, axis=-1, keepdims=True) + EPS)
    return (y * g.astype(F32)).astype(x.dtype)


def layer_norm(x, g, b):
    xf = x.astype(F32)
    mu = jnp.mean(xf, axis=-1, keepdims=True)
    xc = xf - mu
    var = jnp.mean(xc * xc, axis=-1, keepdims=True)
    return (xc * lax.rsqrt(var + EPS) * g.astype(F32) + b.astype(F32)).astype(x.dtype)


def t5_bucket(rel):
    n = jnp.maximum(rel, 0)
    max_exact = T5_BUCKETS // 2
    nf = jnp.maximum(n, 1).astype(F32)
    large = max_exact + (jnp.log(nf / max_exact) / math.log(T5_MAX_DIST / max_exact)
                         * (T5_BUCKETS - max_exact)).astype(jnp.int32)
    large = jnp.minimum(large, T5_BUCKETS - 1)
    return jnp.where(n < max_exact, n, large)


def fox_attention(q, k, v, log_f):
    B, S, H, Dh = q.shape
    nb = S // BLOCK
    scale = Dh ** -0.5
    cum = jnp.cumsum(log_f, axis=1)
    cum_t = cum.transpose(0, 2, 1)
    kpos = jnp.arange(S)
    qb = q.reshape(B, nb, BLOCK, H, Dh).swapaxes(0, 1)
    cb = cum.reshape(B, nb, BLOCK, H).swapaxes(0, 1)

    def one_block(args):
        blk, q_blk, c_blk = args
        s = jnp.einsum('bqhd,bkhd->bhqk', q_blk, k, preferred_element_type=F32) * scale
        s = s + (c_blk.transpose(0, 2, 1)[:, :, :, None] - cum_t[:, :, None, :])
        qpos = blk * BLOCK + jnp.arange(BLOCK)
        causal = kpos[None, :] <= qpos[:, None]
        s = jnp.where(causal[None, None], s, -jnp.inf)
        pr = jax.nn.softmax(s, axis=-1)
        return jnp.einsum('bhqk,bkhd->bqhd', pr.astype(v.dtype), v)

    out = lax.map(one_block, (jnp.arange(nb), qb, cb))
    return out.swapaxes(0, 1).reshape(B, S, H, Dh)


def swa_sink_attention(q, k, v, sinks, band_bias):
    B, S, Hq, Dh = q.shape
    Hkv = k.shape[2]
    G = Hq // Hkv
    nb = S // BLOCK
    qb = q.reshape(B, nb, BLOCK, Hkv, G, Dh)

    def band(t):
        tb = t.reshape(B, nb, BLOCK, Hkv, Dh)
        prev = jnp.pad(tb, ((0, 0), (1, 0), (0, 0), (0, 0), (0, 0)))[:, :-1]
        return jnp.concatenate([prev, tb], axis=2)

    kb, vb = band(k), band(v)
    s = jnp.einsum('bnqhgd,bnkhd->bnhgqk', qb, kb, preferred_element_type=F32) * (Dh ** -0.5)
    s = s + band_bias.reshape(Hkv, G, BLOCK, 2 * BLOCK)
    qi = jnp.arange(BLOCK)[:, None]
    kj = jnp.arange(2 * BLOCK)[None, :]
    rel = qi + BLOCK - kj
    in_window = (rel >= 0) & (rel < SWA_WINDOW)
    has_prev = (jnp.arange(nb)[:, None, None] > 0) | (kj[None] >= BLOCK)
    valid = in_window[None] & has_prev
    s = jnp.where(valid[None, :, None, None], s, -jnp.inf)
    sink_col = jnp.broadcast_to(sinks.astype(F32).reshape(1, 1, Hkv, G, 1, 1), s.shape[:-1] + (1,))
    pr = jax.nn.softmax(jnp.concatenate([s, sink_col], axis=-1), axis=-1)[..., :-1]
    out = jnp.einsum('bnhgqk,bnkhd->bnqhgd', pr.astype(v.dtype), vb)
    return out.reshape(B, S, Hq, Dh)


def dsa_attention(q, c_kv, q_idx, k_idx, w_idx, w_uk, w_uv, t5_c, topk):
    B, S, H, _ = q.shape
    nb = S // BLOCK
    scale = DSA_QK_DIM ** -0.5
    kpos = jnp.arange(S)

    def to_blocks(t):
        return t.reshape((B, nb, BLOCK) + t.shape[2:]).swapaxes(0, 1)

    def one_block(args):
        blk, q_b, qi_b, wi_b = args
        qpos = blk * BLOCK + jnp.arange(BLOCK)
        causal = kpos[None, :] <= qpos[:, None]
        dots = jnp.einsum('bqhd,bkd->bqkh', qi_b, k_idx, preferred_element_type=F32)
        score = jnp.einsum('bqkh,bqh->bqk', jax.nn.relu(dots), wi_b.astype(F32))
        score = jnp.where(causal[None], score, -jnp.inf)
        _, idx = lax.top_k(score, topk)
        lat = jax.vmap(lambda c, i: c[i])(c_kv, idx)
        rel = qpos[None, :, None] - idx
        valid = rel >= 0
        bias = t5_c[t5_bucket(rel)]
        q_abs = jnp.einsum('bqhd,lhd->bqhl', q_b, w_uk)
        s = jnp.einsum('bqhl,bqkl->bqhk', q_abs, lat, preferred_element_type=F32) * scale
        s = s + bias.transpose(0, 1, 3, 2)
        s = jnp.where(valid[:, :, None, :], s, -jnp.inf)
        pr = jax.nn.softmax(s, axis=-1)
        o_lat = jnp.einsum('bqhk,bqkl->bqhl', pr.astype(lat.dtype), lat)
        return jnp.einsum('bqhl,lhd->bqhd', o_lat, w_uv)

    out = lax.map(one_block, (jnp.arange(nb), to_blocks(q), to_blocks(q_idx), to_blocks(w_idx)))
    return out.swapaxes(0, 1).reshape(B, S, H, DSA_V_DIM)


def even_mixer(hn, w_in, b_f, sinks, w_out, band_bias):
    B, S, _ = hn.shape
    proj = hn @ w_in
    q_a, k_a, v_a, f_a, g_a, q_b, k_b, v_b, g_b = _split(proj, EVEN_SIZES)
    log_f = jax.nn.log_sigmoid(f_a.astype(F32) + b_f.astype(F32))
    o_a = fox_attention(q_a.reshape(B, S, FOX_HEADS, FOX_HEAD_DIM),
                        k_a.reshape(B, S, FOX_HEADS, FOX_HEAD_DIM),
                        v_a.reshape(B, S, FOX_HEADS, FOX_HEAD_DIM), log_f)
    o_b = swa_sink_attention(q_b.reshape(B, S, SWA_Q_HEADS, SWA_HEAD_DIM),
                             k_b.reshape(B, S, SWA_KV_HEADS, SWA_HEAD_DIM),
                             v_b.reshape(B, S, SWA_KV_HEADS, SWA_HEAD_DIM), sinks, band_bias)
    y = jnp.concatenate([o_a.reshape(B, S, FOX_WIDTH) * jax.nn.silu(g_a),
                         o_b.reshape(B, S, SWA_WIDTH) * jax.nn.silu(g_b)], axis=-1)
    return y @ w_out


def odd_mixer(hn, w_in, q_norm_g, kv_norm_g, w_uq, w_uq_idx, ln_g, ln_b, w_uk, w_uv, w_out, t5_c, topk):
    B, S, _ = hn.shape
    proj = hn @ w_in
    c_q, c_kv, k_idx, w_idx, g_c = _split(proj, ODD_SIZES)
    c_q = rms_norm(c_q, q_norm_g)
    c_kv = rms_norm(c_kv, kv_norm_g)
    q = (c_q @ w_uq).reshape(B, S, DSA_HEADS, DSA_QK_DIM)
    q_idx = (c_q @ w_uq_idx).reshape(B, S, IDX_HEADS, IDX_DIM)
    k_idx = layer_norm(k_idx, ln_g, ln_b)
    w_idx = w_idx * (IDX_HEADS ** -0.5 * IDX_DIM ** -0.5)
    o = dsa_attention(q, c_kv, q_idx, k_idx, w_idx, w_uk, w_uv, t5_c, topk)
    y = o.reshape(B, S, DSA_WIDTH) * jax.nn.silu(g_c)
    return y @ w_out


def setup_inputs(seed: int = 0) -> dict:
    key = jax.random.key(seed)
    ks = jax.random.split(key, 24)
    n = jax.random.normal
    D = D_MODEL
    return {
        'x': n(ks[0], (BATCH, SEQ, D), F32),
        'p': n(ks[1], (DEPTH, BATCH, SEQ, PLE_DIM), F32),
        't5_table': 0.5 * n(ks[2], (T5_BUCKETS, T5_COLS), F32),
        'norm_g': 1.0 + 0.02 * n(ks[3], (DEPTH, D), F32),
        'even_w_in': n(ks[4], (N_EVEN, D, EVEN_IN), F32) * D ** -0.5,
        'even_b_f': 2.0 + 0.5 * n(ks[5], (N_EVEN, FOX_HEADS), F32),
        'even_sinks': 0.5 * n(ks[6], (N_EVEN, SWA_Q_HEADS), F32),
        'even_w_out': n(ks[7], (N_EVEN, EVEN_OUT, D), F32) * EVEN_OUT ** -0.5,
        'odd_w_in': n(ks[8], (N_ODD, D, ODD_IN), F32) * D ** -0.5,
        'odd_q_norm_g': 1.0 + 0.02 * n(ks[9], (N_ODD, DSA_Q_LATENT), F32),
        'odd_kv_norm_g': 1.0 + 0.02 * n(ks[10], (N_ODD, DSA_KV_LATENT), F32),
        'odd_w_uq': n(ks[11], (N_ODD, DSA_Q_LATENT, DSA_HEADS * DSA_QK_DIM), F32) * DSA_Q_LATENT ** -0.5,
        'odd_w_uq_idx': n(ks[12], (N_ODD, DSA_Q_LATENT, IDX_HEADS * IDX_DIM), F32) * DSA_Q_LATENT ** -0.5,
        'odd_idx_ln_g': 1.0 + 0.02 * n(ks[13], (N_ODD, IDX_DIM), F32),
        'odd_idx_ln_b': 0.02 * n(ks[14], (N_ODD, IDX_DIM), F32),
        'odd_w_uk': n(ks[15], (N_ODD, DSA_KV_LATENT, DSA_HEADS, DSA_QK_DIM), F32) * DSA_KV_LATENT ** -0.5,
        'odd_w_uv': n(ks[16], (N_ODD, DSA_KV_LATENT, DSA_HEADS, DSA_V_DIM), F32) * DSA_KV_LATENT ** -0.5,
        'odd_w_out': n(ks[17], (N_ODD, DSA_WIDTH, D), F32) * DSA_WIDTH ** -0.5,
        'ple_w_proj': n(ks[18], (DEPTH, PLE_DIM, D), F32) * PLE_DIM ** -0.5,
        'ple_norm_g': 1.0 + 0.02 * n(ks[19], (DEPTH, D), F32),
        'ple_w_gate': n(ks[20], (DEPTH, D, D), F32) * D ** -0.5,
        'final_g': 1.0 + 0.02 * n(ks[21], (D,), F32),
    }


def reference(x, p, t5_table, norm_g, even_w_in, even_b_f, even_sinks, even_w_out,
              odd_w_in, odd_q_norm_g, odd_kv_norm_g, odd_w_uq, odd_w_uq_idx, odd_idx_ln_g,
              odd_idx_ln_b, odd_w_uk, odd_w_uv, odd_w_out, ple_w_proj, ple_norm_g,
              ple_w_gate, final_g):
    S = x.shape[1]
    topk = min(TOPK_MAX, S // 4)
    rel_band = jnp.arange(BLOCK)[:, None] + BLOCK - jnp.arange(2 * BLOCK)[None, :]
    band_bias = t5_table[t5_bucket(rel_band)][..., :SWA_Q_HEADS].transpose(2, 0, 1)
    t5_c = t5_table[:, SWA_Q_HEADS:]
    h = x
    for i in range(DEPTH):
        j = i // 2
        hn = rms_norm(h, norm_g[i])
        if i % 2 == 0:
            y = even_mixer(hn, even_w_in[j], even_b_f[j], even_sinks[j], even_w_out[j], band_bias)
        else:
            y = odd_mixer(hn, odd_w_in[j], odd_q_norm_g[j], odd_kv_norm_g[j], odd_w_uq[j],
                          odd_w_uq_idx[j], odd_idx_ln_g[j], odd_idx_ln_b[j], odd_w_uk[j],
                          odd_w_uv[j], odd_w_out[j], t5_c, topk)
        h = h + y
        gate = jax.nn.sigmoid(rms_norm(h, ple_norm_g[i]) @ ple_w_gate[i])
        h = h + gate * (p[i] @ ple_w_proj[i])
    return rms_norm(h, final_g)
```

```python
import numpy as np
import concourse.bass as bass
import concourse.mybir as mybir
from concourse.bass_utils import run_bass_kernel_spmd

F32 = mybir.dt.float32
BF16 = mybir.dt.bfloat16
AF = mybir.ActivationFunctionType
ALU = mybir.AluOpType
AX = mybir.AxisListType


class Tok:
    __slots__ = ("name", "w", "r")

    def __init__(self, name=""):
        self.name = name
        self.w = None
        self.r = []


class Op:
    __slots__ = ("eng", "fn", "deps", "dma", "signal", "count", "sem", "semval", "prev_same_sem")

    def __init__(self, eng, fn, deps, dma):
        self.eng = eng
        self.fn = fn
        self.deps = deps
        self.dma = dma
        self.signal = False
        self.count = 0
        self.sem = None
        self.semval = 0
        self.prev_same_sem = None


class Sched:
    ENGS = ("pe", "act", "dve", "pool", "sp")
    NDMA = 12

    def __init__(self, nc):
        self.nc = nc
        self.ops = {e: [] for e in self.ENGS}
        self.ndma = {e: 0 for e in self.ENGS}
        self.dma_ops = {e: [] for e in self.ENGS}
        self.final_dmas = []

    def op(self, eng, fn, reads=(), writes=(), dma=False, same_ok=False):
        deps = []
        for t in reads:
            if t.w is not None:
                deps.append(t.w)
        for t in writes:
            if t.w is not None:
                deps.append(t.w)
            deps.extend(t.r)
        seen = set()
        d2 = []
        for d in deps:
            if id(d) in seen:
                continue
            seen.add(id(d))
            if same_ok and (not d.dma) and d.eng == eng:
                continue
            d2.append(d)
        rec = Op(eng, fn, d2, dma)
        if dma:
            lst = self.dma_ops[eng]
            n = len(lst)
            if n >= self.NDMA:
                rec.prev_same_sem = lst[n - self.NDMA]
            rec.semval = 16 * (n // self.NDMA + 1)
            rec.sem = (eng, n % self.NDMA)
            lst.append(rec)
        self.ops[eng].append(rec)
        for t in reads:
            t.r.append(rec)
        for t in writes:
            t.w = rec
            t.r = []
        return rec

    def dma(self, eng, out, in_, reads=(), writes=(), final=False, **kw):
        rec = self.op(eng, lambda e: e.dma_start(out=out, in_=in_, **kw), reads, writes, dma=True)
        if final:
            self.final_dmas.append(rec)
        return rec

    def emit(self):
        nc = self.nc
        for e in self.ENGS:
            for o in self.ops[e]:
                for d in o.deps:
                    if not d.dma:
                        d.signal = True
        for e in self.ENGS:
            c = 0
            for o in self.ops[e]:
                if (not o.dma) and o.signal:
                    c += 1
                    o.count = c
        import contextlib
        with contextlib.ExitStack() as st:
            csem = {e: st.enter_context(nc.semaphore("c_" + e)) for e in ("pe", "act", "dve", "pool")}
            dsem = {}
            for e in ("act", "pool", "sp"):
                if self.dma_ops[e]:
                    for i in range(min(self.NDMA, len(self.dma_ops[e]))):
                        dsem[(e, i)] = st.enter_context(nc.semaphore("d_%s%d" % (e, i)))
            block = st.enter_context(nc.Block())
            final = self.final_dmas

            def body(ename):
                def f(eng):
                    known = {}

                    def wait(sem_key, sem, val):
                        if known.get(sem_key, 0) >= val:
                            return
                        eng.wait_ge(sem, val)
                        known[sem_key] = val

                    for o in self.ops[ename]:
                        for d in o.deps:
                            if d.dma:
                                wait(d.sem, dsem[d.sem], d.semval)
                            else:
                                wait(d.eng, csem[d.eng], d.count)
                        if o.dma:
                            if o.prev_same_sem is not None:
                                p = o.prev_same_sem
                                wait(p.sem, dsem[p.sem], p.semval)
                            ins = o.fn(eng)
                            ins.then_inc(dsem[o.sem], 16)
                        else:
                            ins = o.fn(eng)
                            if o.signal:
                                ins.then_inc(csem[ename], 1)
                    if ename == "sp":
                        for d in final:
                            wait(d.sem, dsem[d.sem], d.semval)
                return f

            block.tensor(body("pe"))
            block.scalar(body("act"))
            block.vector(body("dve"))
            block.gpsimd(body("pool"))
            block.sync(body("sp"))

import contextlib

TPC = 1024
GT = 512
D = 4096
EPS = 1e-6


def build_P(cfg):
    nc = bass.Bass("TRN2", target_bir_lowering=False)
    st = contextlib.ExitStack()

    def din(name, shape):
        return nc.dram_tensor(name, shape, F32, kind="ExternalInput").ap()

    def dout(name, shape):
        return nc.dram_tensor(name, shape, F32, kind="ExternalOutput").ap()

    def sb(name, shape, dt):
        return st.enter_context(nc.sbuf_tensor(name, shape, dt))

    def ps(name, shape, dt):
        return st.enter_context(nc.psum_tensor(name, shape, dt))

    ident = din("ident", [128, 128])
    h_in = din("h_in", [TPC, D])
    h_out = dout("h_out", [TPC, D]) if (cfg["outproj"] or cfg["ple"]) else None
    NT_H = D // 512
    if cfg["outproj"]:
        yT = din("yT", [D, TPC])
        w_out = din("w_out", [D, D])
    if cfg["ple"]:
        p_in = din("p", [TPC, 256])
        w_gate = din("w_gate", [D, D])
        w_pp = din("w_pp", [256, D])
        ple_g = din("ple_g", [128, D])
    nxt = cfg["nxt"]
    if nxt:
        NIN = 12816 if nxt == "even" else 5728
        norm_g = din("norm_g", [128, D])
        w_in = din("w_in", [D, NIN])
        proj = dout("proj", [TPC, NIN])
    if nxt == "odd":
        qn_g = din("qn_g", [128, 1024])
        kvn_g = din("kvn_g", [128, 512])
        w_uq = din("w_uq", [1024, 4096])
        w_uqi = din("w_uqi", [1024, 2048])
        w_ukf = din("w_ukf", [512, 4096])
        w_uvf = din("w_uvf", [512, 4096])
        ln_g = din("ln_g", [128, 64])
        ln_b = din("ln_b", [128, 64])
        o_q = dout("o_q", [TPC, 4096])
        o_qi = dout("o_qi", [TPC, 2048])
        o_k = dout("o_k", [TPC, 4096])
        o_v = dout("o_v", [TPC, 4096])
        o_ki = dout("o_ki", [TPC, 64])
        o_wi = dout("o_wi", [TPC, 32])
    if cfg["final"]:
        final_g = din("final_g", [128, D])
        out = dout("out", [TPC, D])

    with st:
        S = Sched(nc)
        idb = sb("idb", [128, 128], BF16)
        t_idb = Tok()
        S.dma("pool", idb[:], ident, writes=[t_idb])
        xrow = [sb("xrow%d" % i, [128, D], F32) for i in range(2)]
        t_xrow = [Tok(), Tok()]
        hnb = [sb("hnb%d" % i, [128, D], BF16) for i in range(2)]
        t_hnb = [Tok(), Tok()]
        gbc = sb("gbc", [128, D], F32)
        t_gbc = Tok()
        xT = sb("xT", [128, 32, GT], BF16)
        t_xT = [Tok() for _ in range(GT // 128)]
        wt = [sb("wt%d" % i, [128, 32, 512], BF16) for i in range(2)]
        t_wt = [Tok(), Tok()]
        ot = [sb("ot%d" % i, [128, 512], F32) for i in range(3)]
        t_ot = [Tok() for _ in range(3)]
        hs = [sb("hs%d" % i, [128, 512], F32) for i in range(3)]
        t_hs = [Tok() for _ in range(3)]
        ss = sb("ss", [128, 8], F32)
        t_ss = Tok()
        epsb = sb("epsb", [128, 1], F32)
        t_eps = Tok()
        S.op("dve", lambda e: e.memset(epsb[:], EPS), writes=[t_eps])
        ptr = [ps("ptr%d" % i, [128, 512], BF16) for i in range(2)]
        t_ptr = [Tok(), Tok()]
        pmm = [ps("pmm%d" % i, [128, 512], F32) for i in range(3)]
        t_pmm = [Tok() for _ in range(3)]
        cnt = {"tr": 0, "mm": 0, "w": 0, "row": 0, "ot": 0, "hs": 0}
        if cfg["ple"]:
            pu = [ps("pu%d" % i, [128, 512], F32) for i in range(2)]
            t_pu = [Tok(), Tok()]
            prow = sb("prow", [128, 256], BF16)
            t_prow = Tok()
            pT = sb("pT", [128, 2, GT], BF16)
            t_pT = [Tok() for _ in range(GT // 128)]
            wpt = [sb("wpt%d" % i, [128, 2, 512], BF16) for i in range(2)]
            t_wpt = [Tok(), Tok()]
            sg = [sb("sg%d" % i, [128, 512], F32) for i in range(2)]
            t_sg = [Tok(), Tok()]
        if nxt == "odd":
            cT = sb("cT", [128, 12, GT], BF16)
            t_cT = [Tok() for _ in range(GT // 128)]
            sm = sb("sm", [128, 256], F32)
            t_sm = Tok()
            lg = sb("lg", [128, 64], F32)
            lb = sb("lb", [128, 64], F32)
            t_lgb = Tok()
            S.dma("sp", lg[:], ln_g, writes=[t_lgb])
            S.dma("sp", lb[:], ln_b, writes=[t_lgb])

        def transposes(src_bf, t_src, nch, dstT, t_dst, tl, ch0=0):
            for g0 in range(0, nch, 4):
                n = min(4, nch - g0)
                pb = cnt["tr"] % 2
                cnt["tr"] += 1

                def f(eng, g0=g0, n=n, pb=pb):
                    for j in range(n):
                        c = g0 + j
                        ins = eng.transpose(ptr[pb][:, j * 128:(j + 1) * 128], src_bf[:, c * 128:(c + 1) * 128], idb[:])
                    return ins
                S.op("pe", f, reads=[t_src, t_idb], writes=[t_ptr[pb]], same_ok=True)
                en = "act" if (cnt["tr"] % 2) else "dve"

                def f2(eng, g0=g0, n=n, pb=pb, en=en):
                    o = dstT[:, ch0 + g0:ch0 + g0 + n, tl * 128:(tl + 1) * 128]
                    i = ptr[pb][:, 0:n * 128].rearrange("p (j t) -> p j t", j=n)
                    if en == "act":
                        return eng.activation(out=o, in_=i, func=AF.Copy)
                    return eng.tensor_copy(out=o, in_=i)
                S.op(en, f2, reads=[t_ptr[pb]], writes=[t_dst])

        def load_gbc(src, width=D):
            S.dma("sp", gbc[:, 0:width], src, writes=[t_gbc])

        def norm_rows(src_rows, src_toks, width, dstT, t_dst, tl, ch0=0):
            b = cnt["row"] % 2
            cnt["row"] += 1
            S.dma("sp", xrow[b][:, 0:width], src_rows, reads=src_toks, writes=[t_xrow[b]])
            col = cnt["row"] % 8
            S.op("act", lambda e: e.activation(out=hnb[b][:, 0:width], in_=xrow[b][:, 0:width], func=AF.Square,
                                              scale=float(width) ** -0.5, accum_out=ss[:, col:col + 1]),
                 reads=[t_xrow[b]], writes=[t_hnb[b], t_ss])
            S.op("act", lambda e: e.activation(out=ss[:, col:col + 1], in_=ss[:, col:col + 1], func=AF.Sqrt, bias=epsb[:]), reads=[t_ss, t_eps], writes=[t_ss])
            S.op("dve", lambda e: e.reciprocal(out=ss[:, col:col + 1], in_=ss[:, col:col + 1]), reads=[t_ss], writes=[t_ss])
            S.op("dve", lambda e: e.scalar_tensor_tensor(out=hnb[b][:, 0:width], in0=xrow[b][:, 0:width], scalar=ss[:, col:col + 1],
                                                        in1=gbc[:, 0:width], op0=ALU.mult, op1=ALU.mult),
                 reads=[t_xrow[b], t_ss, t_gbc], writes=[t_hnb[b]])
            transposes(hnb[b], t_hnb[b], width // 128, dstT, t_dst, tl, ch0)

        def final_rows(src, toks, dst):
            b = cnt["row"] % 2
            cnt["row"] += 1
            col = cnt["row"] % 8
            S.dma("sp", xrow[b][:], src, reads=toks, writes=[t_xrow[b]])
            S.op("act", lambda e: e.activation(out=hnb[b][:], in_=xrow[b][:], func=AF.Square, scale=float(D) ** -0.5,
                                              accum_out=ss[:, col:col + 1]), reads=[t_xrow[b]], writes=[t_hnb[b], t_ss])
            S.op("act", lambda e: e.activation(out=ss[:, col:col + 1], in_=ss[:, col:col + 1], func=AF.Sqrt, bias=epsb[:]), reads=[t_ss, t_eps], writes=[t_ss])
            S.op("dve", lambda e: e.reciprocal(out=ss[:, col:col + 1], in_=ss[:, col:col + 1]), reads=[t_ss], writes=[t_ss])
            S.op("dve", lambda e: e.scalar_tensor_tensor(out=xrow[b][:], in0=xrow[b][:], scalar=ss[:, col:col + 1],
                                                        in1=gbc[:], op0=ALU.mult, op1=ALU.mult),
                 reads=[t_xrow[b], t_ss, t_gbc], writes=[t_xrow[b]])
            S.dma("sp", dst, xrow[b][:], reads=[t_xrow[b]], final=True)

        def linear(xTbuf, t_x, kch0, nk, W, ncols, epilogue, ntl=GT // 128, extra=None):
            for n0 in range(0, ncols, 512):
                nw = min(512, ncols - n0)
                wb = cnt["w"] % 2
                cnt["w"] += 1
                S.dma("pool", wt[wb][:, 0:nk, 0:nw], W[:, n0:n0 + nw].rearrange("(c p) n -> p c n", p=128), writes=[t_wt[wb]])
                ex = extra(n0, nw) if extra else None
                for tl in range(ntl):
                    pb = cnt["mm"] % 3
                    cnt["mm"] += 1

                    def f(eng, wb=wb, tl=tl, pb=pb, nw=nw):
                        for c in range(nk):
                            ins = eng.matmul(pmm[pb][:, 0:nw], lhsT=xTbuf[:, kch0 + c, tl * 128:(tl + 1) * 128],
                                             rhs=wt[wb][:, c, 0:nw], start=(c == 0), stop=(c == nk - 1))
                        return ins
                    S.op("pe", f, reads=[t_x[tl], t_wt[wb]], writes=[t_pmm[pb]], same_ok=True)
                    epilogue(tl, n0, nw, pmm[pb], t_pmm[pb], ex)

        def store_epi(dst, t0):
            def epi(tl, n0, nw, pm, t_pm, ex):
                k = cnt["ot"] % 3
                cnt["ot"] += 1
                en = "act" if k % 2 else "dve"
                if en == "act":
                    S.op("act", lambda e: e.activation(out=ot[k][:, 0:nw], in_=pm[:, 0:nw], func=AF.Copy), reads=[t_pm], writes=[t_ot[k]])
                else:
                    S.op("dve", lambda e: e.tensor_copy(out=ot[k][:, 0:nw], in_=pm[:, 0:nw]), reads=[t_pm], writes=[t_ot[k]])
                S.dma("sp", dst[t0 + tl * 128:t0 + (tl + 1) * 128, n0:n0 + nw], ot[k][:, 0:nw], reads=[t_ot[k]], final=True)
            return epi

        for grp in range(TPC // GT):
            t0 = grp * GT
            t_h = [[Tok() for _ in range(NT_H)] for _ in range(GT // 128)]
            hsrc = h_in
            if cfg["outproj"]:
                S.dma("pool", xT[:], yT[:, t0:t0 + GT].rearrange("(c p) t -> p c t", p=128), writes=t_xT)

                def epi1(tl, n0, nw, pm, t_pm, ex):
                    k = cnt["hs"] % 3
                    cnt["hs"] += 1
                    rows = slice(t0 + tl * 128, t0 + (tl + 1) * 128)
                    S.dma("sp", hs[k][:], h_in[rows, n0:n0 + nw], writes=[t_hs[k]])
                    S.op("dve", lambda e: e.tensor_tensor(out=hs[k][:], in0=pm[:], in1=hs[k][:], op=ALU.add), reads=[t_pm, t_hs[k]], writes=[t_hs[k]])
                    S.dma("sp", h_out[rows, n0:n0 + nw], hs[k][:], reads=[t_hs[k]], writes=[t_h[tl][n0 // 512]], final=True)
                linear(xT, t_xT, 0, 32, w_out, D, epi1)
                hsrc = h_out
            if cfg["ple"]:
                load_gbc(ple_g)
                for tl in range(GT // 128):
                    rows = slice(t0 + tl * 128, t0 + (tl + 1) * 128)
                    norm_rows(hsrc[rows, :], t_h[tl], D, xT, t_xT[tl], tl)
                    S.dma("pool", prow[:], p_in[rows, :], writes=[t_prow])
                    transposes(prow, t_prow, 2, pT, t_pT[tl], tl)

                def extra2(n0, nw):
                    wb = (n0 // 512) % 2
                    S.dma("pool", wpt[wb][:], w_pp[:, n0:n0 + nw].rearrange("(c p) n -> p c n", p=128), writes=[t_wpt[wb]])
                    return wb

                def epi2(tl, n0, nw, pm, t_pm, wb):
                    ub = cnt["hs"] % 2
                    k = cnt["hs"] % 3
                    cnt["hs"] += 1
                    rows = slice(t0 + tl * 128, t0 + (tl + 1) * 128)

                    def fu(eng):
                        for c in range(2):
                            ins = eng.matmul(pu[ub][:], lhsT=pT[:, c, tl * 128:(tl + 1) * 128], rhs=wpt[wb][:, c, :], start=(c == 0), stop=(c == 1))
                        return ins
                    S.op("pe", fu, reads=[t_pT[tl], t_wpt[wb]], writes=[t_pu[ub]], same_ok=True)
                    S.dma("sp", hs[k][:], hsrc[rows, n0:n0 + nw], reads=[t_h[tl][n0 // 512]], writes=[t_hs[k]])
                    S.op("act", lambda e: e.activation(out=sg[ub][:], in_=pm[:], func=AF.Sigmoid), reads=[t_pm], writes=[t_sg[ub]])
                    S.op("dve", lambda e: e.tensor_tensor(out=sg[ub][:], in0=pu[ub][:], in1=sg[ub][:], op=ALU.mult), reads=[t_pu[ub], t_sg[ub]], writes=[t_sg[ub]])
                    S.op("dve", lambda e: e.tensor_tensor(out=hs[k][:], in0=sg[ub][:], in1=hs[k][:], op=ALU.add), reads=[t_sg[ub], t_hs[k]], writes=[t_hs[k]])
                    S.dma("sp", h_out[rows, n0:n0 + nw], hs[k][:], reads=[t_hs[k]], writes=[t_h[tl][n0 // 512]], final=True)
                linear(xT, t_xT, 0, 32, w_gate, D, epi2, extra=extra2)
                hsrc = h_out
            if nxt:
                load_gbc(norm_g)
                for tl in range(GT // 128):
                    rows = slice(t0 + tl * 128, t0 + (tl + 1) * 128)
                    norm_rows(hsrc[rows, :], t_h[tl], D, xT, t_xT[tl], tl)
                if nxt == "even":
                    linear(xT, t_xT, 0, 32, w_in, NIN, store_epi(proj, t0))
                else:
                    t_pr = [[Tok() for _ in range(4)] for _ in range(GT // 128)]

                    def epi3(tl, n0, nw, pm, t_pm, ex):
                        k = cnt["ot"] % 3
                        cnt["ot"] += 1
                        S.op("dve", lambda e: e.tensor_copy(out=ot[k][:, 0:nw], in_=pm[:, 0:nw]), reads=[t_pm], writes=[t_ot[k]])
                        wr = [t_pr[tl][n0 // 512]] if n0 < 2048 else []
                        S.dma("sp", proj[t0 + tl * 128:t0 + (tl + 1) * 128, n0:n0 + nw], ot[k][:, 0:nw], reads=[t_ot[k]], writes=wr, final=True)
                    linear(xT, t_xT, 0, 32, w_in, NIN, epi3)
                    load_gbc(qn_g, 1024)
                    for tl in range(GT // 128):
                        rows = slice(t0 + tl * 128, t0 + (tl + 1) * 128)
                        norm_rows(proj[rows, 0:1024], t_pr[tl][0:2], 1024, cT, t_cT[tl], tl, 0)
                    load_gbc(kvn_g, 512)
                    for tl in range(GT // 128):
                        rows = slice(t0 + tl * 128, t0 + (tl + 1) * 128)
                        norm_rows(proj[rows, 1024:1536], t_pr[tl][2:3], 512, cT, t_cT[tl], tl, 8)
                        S.dma("sp", sm[:, 0:96], proj[rows, 1536:1632], reads=t_pr[tl][3:4], writes=[t_sm])
                        S.op("dve", lambda e: e.reduce_sum(out=sm[:, 100:101], in_=sm[:, 0:64], axis=AX.X), reads=[t_sm], writes=[t_sm])
                        S.op("dve", lambda e: e.tensor_scalar(out=sm[:, 100:101], in0=sm[:, 100:101], scalar1=-1.0 / 64, scalar2=None, op0=ALU.mult), reads=[t_sm], writes=[t_sm])
                        S.op("dve", lambda e: e.tensor_scalar(out=sm[:, 0:64], in0=sm[:, 0:64], scalar1=sm[:, 100:101], scalar2=None, op0=ALU.add), reads=[t_sm], writes=[t_sm])
                        S.op("act", lambda e: e.activation(out=sm[:, 128:192], in_=sm[:, 0:64], func=AF.Square, scale=0.125, accum_out=sm[:, 101:102]), reads=[t_sm], writes=[t_sm])
                        S.op("act", lambda e: e.activation(out=sm[:, 101:102], in_=sm[:, 101:102], func=AF.Sqrt, bias=epsb[:]), reads=[t_sm, t_eps], writes=[t_sm])
                        S.op("dve", lambda e: e.reciprocal(out=sm[:, 101:102], in_=sm[:, 101:102]), reads=[t_sm], writes=[t_sm])
                        S.op("dve", lambda e: e.scalar_tensor_tensor(out=sm[:, 0:64], in0=sm[:, 0:64], scalar=sm[:, 101:102], in1=lg[:], op0=ALU.mult, op1=ALU.mult), reads=[t_sm, t_lgb], writes=[t_sm])
                        S.op("dve", lambda e: e.tensor_tensor(out=sm[:, 0:64], in0=sm[:, 0:64], in1=lb[:], op=ALU.add), reads=[t_sm, t_lgb], writes=[t_sm])
                        S.op("dve", lambda e: e.tensor_scalar(out=sm[:, 64:96], in0=sm[:, 64:96], scalar1=float(32 ** -0.5 * 64 ** -0.5), scalar2=None, op0=ALU.mult), reads=[t_sm], writes=[t_sm])
                        S.dma("sp", o_ki[rows, :], sm[:, 0:64], reads=[t_sm], final=True)
                        S.dma("sp", o_wi[rows, :], sm[:, 64:96], reads=[t_sm], final=True)
                    linear(cT, t_cT, 0, 8, w_uq, 4096, store_epi(o_q, t0))
                    linear(cT, t_cT, 0, 8, w_uqi, 2048, store_epi(o_qi, t0))
                    linear(cT, t_cT, 8, 4, w_ukf, 4096, store_epi(o_k, t0))
                    linear(cT, t_cT, 8, 4, w_uvf, 4096, store_epi(o_v, t0))
            if cfg["final"]:
                load_gbc(final_g)
                for tl in range(GT // 128):
                    rows = slice(t0 + tl * 128, t0 + (tl + 1) * 128)
                    final_rows(hsrc[rows, :], t_h[tl], out[rows, :])
        S.emit()
    return nc

import contextlib

SQ = 4096
NB = SQ // 128
R128 = float(128 ** 0.5)


def build_E():
    nc = bass.Bass("TRN2", target_bir_lowering=False)
    st = contextlib.ExitStack()

    def din(name, shape):
        return nc.dram_tensor(name, shape, F32, kind="ExternalInput").ap()

    def dout(name, shape):
        return nc.dram_tensor(name, shape, F32, kind="ExternalOutput").ap()

    def sb(name, shape, dt):
        return st.enter_context(nc.sbuf_tensor(name, shape, dt))

    def ps(name, shape, dt):
        return st.enter_context(nc.psum_tensor(name, shape, dt))

    qTa = din("qTa", [4, 128, SQ])
    kTa = din("kTa", [4, 128, SQ])
    va = din("va", [4, 128, NB * 128])
    gTa = din("gTa", [4, 128, SQ])
    fT = din("fT", [4, SQ])
    bf = din("bf", [4, 1])
    sel = din("sel", [4, 4 * 128])
    id4 = din("id4", [4, 4])
    triT = din("triT", [128, 128])
    yTa = dout("yTa", [4, 128, SQ])
    qTb = din("qTb", [64, 8, SQ])
    kTb = din("kTb", [64, SQ])
    vb = din("vb", [128, NB * 64])
    gTb = din("gTb", [64, 8, SQ])
    bandT = din("bandT", [128, 8 * 2 * 128])
    sinkb = din("sinkb", [64, 8])
    yTb = dout("yTb", [64, 8, SQ])

    with st:
        S = Sched(nc)
        ones_bf = sb("ones_bf", [128, 128], BF16)
        t_ones = Tok()
        S.op("dve", lambda e: e.memset(ones_bf[:], 1.0), writes=[t_ones])
        tri = sb("tri", [128, 128], BF16)
        t_tri = Tok()
        S.dma("pool", tri[:], triT, writes=[t_tri])
        selt = sb("selt", [4, 512], F32)
        id4t = sb("id4t", [4, 4], F32)
        t_sel = Tok()
        S.dma("sp", selt[:], sel, writes=[t_sel])
        S.dma("sp", id4t[:], id4, writes=[t_sel])
        f4 = sb("f4", [4, SQ], F32)
        e4 = sb("e4", [4, SQ], F32)
        ones4 = sb("ones4", [4, 512], F32)
        nb = sb("nb", [4, 1], F32)
        t_f4 = Tok()
        t_e4 = Tok()
        t_nb = Tok()
        t_o4 = Tok()
        S.dma("sp", f4[:], fT, writes=[t_f4])
        S.dma("sp", nb[:], bf, writes=[t_nb])
        S.op("dve", lambda e: e.memset(ones4[:], 1.0), writes=[t_o4])
        S.op("dve", lambda e: e.tensor_scalar(out=nb[:], in0=nb[:], scalar1=-1.0, scalar2=None, op0=ALU.mult), reads=[t_nb], writes=[t_nb])
        S.op("act", lambda e: e.activation(out=e4[:], in_=f4[:], func=AF.Exp, scale=-1.0, bias=nb[:, 0:1]), reads=[t_f4, t_nb], writes=[t_e4])
        S.op("act", lambda e: e.activation(out=f4[:], in_=e4[:], func=AF.Ln, scale=1.0, bias=1.0), reads=[t_e4], writes=[t_f4])
        for c in range(SQ // 512):
            S.op("dve", lambda e, c=c: e.tensor_tensor_scan(out=e4[:, c * 512:(c + 1) * 512], data0=ones4[:], data1=f4[:, c * 512:(c + 1) * 512],
                                                          initial=(0.0 if c == 0 else e4[:, c * 512 - 1:c * 512]), op0=ALU.mult, op1=ALU.add),
                 reads=[t_f4, t_o4, t_e4], writes=[t_e4])
        csp = e4
        t_csp = t_e4
        pst = ps("pst", [128, 512], F32)
        t_pst = Tok()
        acol = sb("acol", [128, NB * 4], F32)
        t_acol = Tok()

        def f_tr(eng):
            for blk in range(NB):
                ins = eng.transpose(pst[:, blk * 4:(blk + 1) * 4], csp[0:4, blk * 128:(blk + 1) * 128], id4t[0:4, 0:4])
            return ins
        S.op("pe", f_tr, reads=[t_csp, t_sel], writes=[t_pst], same_ok=True)
        S.op("act", lambda e: e.activation(out=acol[:], in_=pst[:, 0:NB * 4], func=AF.Copy, scale=R128), reads=[t_pst], writes=[t_acol])

        negcb = sb("negcb", [128, SQ], F32)
        t_negcb = Tok()
        kT = [sb("kT%d" % i, [128, SQ], BF16) for i in range(2)]
        qT = [sb("qT%d" % i, [128, SQ], BF16) for i in range(2)]
        vt = [sb("vt%d" % i, [128, NB, 128], BF16) for i in range(2)]
        t_kT = [Tok(), Tok()]
        t_qT = [Tok(), Tok()]
        t_vt = [Tok(), Tok()]
        pS = [ps("pS%d" % i, [128, 512], F32) for i in range(3)]
        t_pS = [Tok(), Tok(), Tok()]
        pO = [ps("pO%d" % i, [128, 512], F32) for i in range(2)]
        t_pO = [Tok(), Tok()]
        pD = [ps("pD%d" % i, [128, 512], F32) for i in range(2)]
        t_pD = [Tok(), Tok()]
        tmp = [sb("tmp%d" % i, [128, 512], F32) for i in range(3)]
        t_tmp = [Tok(), Tok(), Tok()]
        PT = [sb("PT%d" % i, [128, 512], BF16) for i in range(4)]
        t_PT = [Tok() for _ in range(4)]
        g32 = [sb("g32_%d" % i, [128, 512], F32) for i in range(2)]
        t_g32 = [Tok(), Tok()]
        rd = [sb("rd%d" % i, [128, 512], F32) for i in range(2)]
        t_rd = [Tok(), Tok()]
        yt = [sb("yt%d" % i, [128, 512], F32) for i in range(2)]
        t_yt = [Tok(), Tok()]

        def head_setup(h):
            hb = h % 2
            S.dma("pool", kT[hb][:], kTa[h], writes=[t_kT[hb]])
            S.dma("pool", qT[hb][:], qTa[h], writes=[t_qT[hb]])
            S.dma("pool", vt[hb][:].rearrange("p j d -> p (j d)"), va[h], writes=[t_vt[hb]])

        def negcb_setup(h):
            for c in range(SQ // 512):
                S.op("pe", lambda e, c=c: e.matmul(pst[:], lhsT=selt[0:4, h * 128:(h + 1) * 128], rhs=csp[0:4, c * 512:(c + 1) * 512], start=True, stop=True),
                     reads=[t_csp, t_sel], writes=[t_pst])
                S.op("act", lambda e, c=c: e.activation(out=negcb[:, c * 512:(c + 1) * 512], in_=pst[:], func=AF.Copy, scale=-R128),
                     reads=[t_pst], writes=[t_negcb])

        items = []
        for h in range(4):
            for Q in range(SQ // 512):
                for j in range(4 * Q + 4):
                    items.append((h, Q, j))
        cnt = {"s": 0, "pt": 0, "fin": 0}
        state = {}

        def emit_qk(it):
            h, Q, j = it
            hb = h % 2
            c0 = 0 if j < 4 * Q else (j - 4 * Q) * 128
            sbuf = cnt["s"] % 3
            cnt["s"] += 1
            state[it] = sbuf
            S.op("pe", lambda e: e.matmul(pS[sbuf][:, c0:512], lhsT=kT[hb][:, j * 128:(j + 1) * 128],
                                          rhs=qT[hb][:, Q * 512 + c0:(Q + 1) * 512], start=True, stop=True),
                 reads=[t_kT[hb], t_qT[hb]], writes=[t_pS[sbuf]])

        def emit_rest(it):
            h, Q, j = it
            hb = h % 2
            c0 = 0 if j < 4 * Q else (j - 4 * Q) * 128
            sbuf = state.pop(it)
            ob = (h * 8 + Q) % 2
            pk = cnt["pt"] % 4
            cnt["pt"] += 1
            last = (j == 4 * Q + 3)
            S.op("dve", lambda e: e.scalar_tensor_tensor(out=tmp[sbuf][:, c0:512], in0=pS[sbuf][:, c0:512], scalar=acol[:, j * 4 + h:j * 4 + h + 1],
                                                        in1=negcb[:, Q * 512 + c0:(Q + 1) * 512], op0=ALU.add, op1=ALU.add),
                 reads=[t_pS[sbuf], t_acol, t_negcb], writes=[t_tmp[sbuf]])
            S.op("act", lambda e: e.activation(out=PT[pk][:, c0:512], in_=tmp[sbuf][:, c0:512], func=AF.Exp, scale=1.0 / R128),
                 reads=[t_tmp[sbuf]], writes=[t_PT[pk]])
            if j >= 4 * Q:
                S.op("dve", lambda e: e.tensor_tensor(out=PT[pk][:, c0:c0 + 128], in0=PT[pk][:, c0:c0 + 128], in1=tri[:], op=ALU.mult),
                     reads=[t_PT[pk], t_tri], writes=[t_PT[pk]])

            def f(eng):
                eng.matmul(pO[ob][:, c0:512], lhsT=vt[hb][:, j, :], rhs=PT[pk][:, c0:512], start=(j == 0), stop=last)
                return eng.matmul(pD[ob][:, c0:512], lhsT=ones_bf[:], rhs=PT[pk][:, c0:512], start=(j == 0), stop=last)
            S.op("pe", f, reads=[t_PT[pk], t_vt[hb], t_ones], writes=[t_pO[ob], t_pD[ob]], same_ok=True)
            if last:
                k = cnt["fin"] % 2
                cnt["fin"] += 1
                S.dma("sp", g32[k][:], gTa[h, :, Q * 512:(Q + 1) * 512], writes=[t_g32[k]])
                S.op("act", lambda e: e.activation(out=g32[k][:], in_=g32[k][:], func=AF.Silu), reads=[t_g32[k]], writes=[t_g32[k]])
                S.op("dve", lambda e: e.reciprocal(out=rd[k][:], in_=pD[ob][:]), reads=[t_pD[ob]], writes=[t_rd[k]])
                S.op("dve", lambda e: e.tensor_tensor(out=rd[k][:], in0=pO[ob][:], in1=rd[k][:], op=ALU.mult), reads=[t_pO[ob], t_rd[k]], writes=[t_rd[k]])
                S.op("dve", lambda e: e.tensor_tensor(out=yt[k][:], in0=rd[k][:], in1=g32[k][:], op=ALU.mult), reads=[t_rd[k], t_g32[k]], writes=[t_yt[k]])
                S.dma("sp", yTa[h, :, Q * 512:(Q + 1) * 512], yt[k][:], reads=[t_yt[k]], final=True)

        head_setup(0)
        negcb_setup(0)
        emit_qk(items[0])
        emit_qk(items[1])
        for n, it in enumerate(items):
            h, Q, j = it
            if Q == 0 and j == 0 and h + 1 < 4:
                head_setup(h + 1)
            nx = items[n + 1] if n + 1 < len(items) else None
            nx2 = items[n + 2] if n + 2 < len(items) else None
            if nx2 is not None:
                emit_qk(nx2)
            emit_rest(it)
            if nx is not None and nx[0] != h:
                negcb_setup(nx[0])

        EB = sb("EB", [128, 2048], BF16)
        t_EB = Tok()
        S.dma("sp", negcb[:, 0:2048], bandT, writes=[t_negcb])
        S.op("act", lambda e: e.activation(out=EB[:], in_=negcb[:, 0:2048], func=AF.Exp), reads=[t_negcb], writes=[t_EB])
        sk8 = sb("sk8", [64, 8], F32)
        SK = sb("SK", [64, 8 * 128], F32)
        t_SK = Tok()
        S.dma("sp", sk8[:], sinkb, writes=[t_SK])
        S.op("act", lambda e: e.activation(out=sk8[:], in_=sk8[:], func=AF.Exp), reads=[t_SK], writes=[t_SK])
        S.op("dve", lambda e: e.memset(SK[:], 0.0), writes=[t_SK])
        for hh in range(8):
            S.op("dve", lambda e, hh=hh: e.tensor_scalar(out=SK[:, hh * 128:(hh + 1) * 128], in0=SK[:, hh * 128:(hh + 1) * 128],
                                                        scalar1=sk8[:, hh:hh + 1], scalar2=None, op0=ALU.add), reads=[t_SK], writes=[t_SK])
        kTs = sb("kTs", [64, SQ], BF16)
        vs = sb("vs", [128, NB, 64], BF16)
        t_kvs = Tok()
        S.dma("pool", kTs[:], kTb, writes=[t_kvs])
        S.dma("pool", vs[:].rearrange("p j d -> p (j d)"), vb, writes=[t_kvs])
        qTs = sb("qTs", [64, 4, SQ], BF16)
        t_qTs = Tok()
        gs = [sb("gs0", [64, 4, 512], F32)] * 2
        t_gs = [Tok()] * 2
        Es = [sb("Es%d" % i, [128, 512], BF16) for i in range(2)]
        t_Es = [Tok(), Tok()]
        den = [sb("den%d" % i, [64, 512], F32) for i in range(2)]
        t_den = [Tok(), Tok()]
        ys = [sb("ys%d" % i, [64, 4, 128], F32) for i in range(2)]
        t_ys = [Tok(), Tok()]
        c2 = {"s": 0, "o": 0, "g": 0}
        for hg in range(2):
            S.dma("pool", qTs[:], qTb[:, hg * 4:(hg + 1) * 4, :], writes=[t_qTs])
            for i in range(NB):
                if i % 4 == 0:
                    gk = c2["g"] % 2
                    c2["g"] += 1
                    S.dma("sp", gs[gk][:], gTb[:, hg * 4:(hg + 1) * 4, i * 128:(i + 4) * 128], writes=[t_gs[gk]])
                    S.op("act", lambda e, gk=gk: e.activation(out=gs[gk][:], in_=gs[gk][:], func=AF.Silu), reads=[t_gs[gk]], writes=[t_gs[gk]])
                ob = c2["o"] % 2
                c2["o"] += 1
                parts = ([(i - 1, 0)] if i > 0 else []) + [(i, 1)]
                for pi, (kb, which) in enumerate(parts):
                    sbuf = c2["s"] % 2
                    c2["s"] += 1

                    def fqk(eng, kb=kb, sbuf=sbuf, i=i):
                        for hh in range(4):
                            ins = eng.matmul(pS[sbuf][:, hh * 128:(hh + 1) * 128], lhsT=kTs[:, kb * 128:(kb + 1) * 128],
                                             rhs=qTs[:, hh, i * 128:(i + 1) * 128], start=True, stop=True)
                        return ins
                    S.op("pe", fqk, reads=[t_kvs, t_qTs], writes=[t_pS[sbuf]], same_ok=True)
                    S.op("act", lambda e, sbuf=sbuf: e.activation(out=Es[sbuf][:], in_=pS[sbuf][:], func=AF.Exp, scale=0.125),
                         reads=[t_pS[sbuf]], writes=[t_Es[sbuf]])

                    def fmul(eng, sbuf=sbuf, which=which, hg=hg):
                        ebv = EB[:].rearrange("p (h w q) -> p h w q", h=8, w=2)[:, hg * 4:(hg + 1) * 4, which, :]
                        return eng.tensor_tensor(out=Es[sbuf][:].rearrange("p (h q) -> p h q", h=4), in0=Es[sbuf][:].rearrange("p (h q) -> p h q", h=4),
                                                 in1=ebv, op=ALU.mult)
                    S.op("dve", fmul, reads=[t_Es[sbuf], t_EB], writes=[t_Es[sbuf]])

                    def fpv(eng, kb=kb, sbuf=sbuf, pi=pi, np_=len(parts), ob=ob):
                        eng.matmul(pO[ob][0:64, :], lhsT=vs[:, kb, :], rhs=Es[sbuf][:], start=(pi == 0), stop=(pi == np_ - 1))
                        return eng.matmul(pD[ob][0:64, :], lhsT=ones_bf[:, 0:64], rhs=Es[sbuf][:], start=(pi == 0), stop=(pi == np_ - 1))
                    S.op("pe", fpv, reads=[t_Es[sbuf], t_kvs, t_ones], writes=[t_pO[ob], t_pD[ob]], same_ok=True)
                k = ob
                S.op("dve", lambda e, k=k, hg=hg: e.tensor_tensor(out=den[k][:], in0=pD[k][0:64, :], in1=SK[:, hg * 512:(hg + 1) * 512], op=ALU.add),
                     reads=[t_pD[k], t_SK], writes=[t_den[k]])
                S.op("dve", lambda e, k=k: e.reciprocal(out=den[k][:], in_=den[k][:]), reads=[t_den[k]], writes=[t_den[k]])
                S.op("dve", lambda e, k=k: e.tensor_tensor(out=den[k][:], in0=pO[k][0:64, :], in1=den[k][:], op=ALU.mult),
                     reads=[t_pO[k], t_den[k]], writes=[t_den[k]])
                gk = (c2["g"] - 1) % 2

                def fy(eng, k=k, gk=gk, i=i):
                    return eng.tensor_tensor(out=ys[k][:], in0=den[k][:].rearrange("p (h q) -> p h q", h=4),
                                             in1=gs[gk][:, :, (i % 4) * 128:(i % 4 + 1) * 128], op=ALU.mult)
                S.op("dve", fy, reads=[t_den[k], t_gs[gk]], writes=[t_ys[k]])
                S.dma("sp", yTb[:, hg * 4:(hg + 1) * 4, i * 128:(i + 1) * 128], ys[k][:], reads=[t_ys[k]], final=True)
        S.emit()
    return nc

import contextlib

SQ = 4096
NB = SQ // 128
NS = 8
NH = 32
NQ = NS * 128
SCALE = float(128 ** -0.5)
NEG_SEL = -3.0e38


def build_O(debug=False):
    nc = bass.Bass("TRN2", target_bir_lowering=False)
    st = contextlib.ExitStack()

    def din(name, shape):
        return nc.dram_tensor(name, shape, F32, kind="ExternalInput").ap()

    def dout(name, shape):
        return nc.dram_tensor(name, shape, F32, kind="ExternalOutput").ap()

    def sb(name, shape, dt):
        return st.enter_context(nc.sbuf_tensor(name, shape, dt))

    def ps(name, shape, dt):
        return st.enter_context(nc.psum_tensor(name, shape, dt))

    ident = din("ident", [128, 128])
    qiT = din("qiT", [NS, 64, NH * 128])
    kiT = din("kiT", [64, SQ])
    wi = din("wi", [128, NS * NH])
    cb = din("cb", [NS, 128, 512])
    QT = din("QT", [NH, 128, NQ])
    KT = din("KT", [NH, 128, SQ])
    V = din("V", [NH, 128, NB * 128])
    gT = din("gT", [NH, 128, NQ])
    rawb = din("rawb", [NH, 128, NS * 5 * 128])
    c31 = din("c31", [128, NH])
    yTo = dout("yTo", [NH, 128, NQ])
    dbg = dout("dbg", [NS, 128, SQ]) if debug else None
    dbg2 = dout("dbg2", [NS, 128, SQ]) if debug else None

    with st:
        S = Sched(nc)
        idf = sb("idf", [128, 128], F32)
        t_id = Tok()
        S.dma("sp", idf[:], ident, writes=[t_id])
        ones_bf = sb("ones_bf", [128, 128], BF16)
        t_ones = Tok()
        S.op("dve", lambda e: e.memset(ones_bf[:], 1.0), writes=[t_ones])
        c31t = sb("c31t", [128, NH], F32)
        t_c31 = Tok()
        S.dma("sp", c31t[:], c31, writes=[t_c31])
        nc31t = sb("nc31t", [128, NH], F32)
        S.op("dve", lambda e: e.tensor_scalar(out=nc31t[:], in0=c31t[:], scalar1=-1.0, scalar2=None, op0=ALU.mult), reads=[t_c31], writes=[t_c31])
        wit = sb("wit", [128, NS * NH], F32)
        t_wi = Tok()
        S.dma("sp", wit[:], wi, writes=[t_wi])
        maskT = sb("maskT", [128, NB, NQ], BF16)
        t_mT = [Tok() for _ in range(NS)]
        pS = [ps("pS%d" % i, [128, 1024], F32) for i in range(2)]
        t_pS = [Tok(), Tok()]
        pO = ps("pO", [128, 1024], F32)
        t_pO = Tok()
        pD = ps("pD", [128, 1024], F32)
        t_pD = Tok()
        kit = sb("kit", [64, SQ], BF16)
        t_kit = Tok()
        S.dma("pool", kit[:], kiT, writes=[t_kit])
        qit = [sb("qit%d" % i, [64, NH * 128], BF16) for i in range(2)]
        t_qit = [Tok(), Tok()]
        acc = sb("acc", [128, SQ], F32)
        t_acc = Tok()
        rl = [sb("rl%d" % i, [128, 512], F32) for i in range(2)]
        t_rl = [Tok(), Tok()]
        cbt = [sb("cbt%d" % i, [128, 512], F32) for i in range(2)]
        t_cbt = [Tok(), Tok()]
        mx = sb("mx", [128, 8], F32)
        t_mx = Tok()
        c1 = {"d": 0}
        for i in range(NS):
            n_i = 512 * (i + 1)
            qb = i % 2
            S.dma("pool", qit[qb][:], qiT[i], writes=[t_qit[qb]])
            S.dma("sp", cbt[qb][:], cb[i], writes=[t_cbt[qb]])
            for ch in range(i + 1):
                for h in range(NH):
                    db = c1["d"] % 2
                    c1["d"] += 1
                    S.op("pe", lambda e, db=db, h=h, ch=ch, qb=qb: e.matmul(pS[db][:, 0:512], lhsT=qit[qb][:, h * 128:(h + 1) * 128],
                                                                          rhs=kit[:, ch * 512:(ch + 1) * 512], start=True, stop=True),
                         reads=[t_qit[qb], t_kit], writes=[t_pS[db]])
                    S.op("act", lambda e, db=db: e.activation(out=rl[db][:], in_=pS[db][:, 0:512], func=AF.Relu), reads=[t_pS[db]], writes=[t_rl[db]])
                    wcol = wit[:, i * NH + h:i * NH + h + 1]
                    if h == 0:
                        S.op("dve", lambda e, db=db, ch=ch, wcol=wcol: e.tensor_scalar(out=acc[:, ch * 512:(ch + 1) * 512], in0=rl[db][:], scalar1=wcol,
                                                                                      scalar2=None, op0=ALU.mult),
                             reads=[t_rl[db], t_wi], writes=[t_acc])
                    else:
                        S.op("dve", lambda e, db=db, ch=ch, wcol=wcol: e.scalar_tensor_tensor(out=acc[:, ch * 512:(ch + 1) * 512], in0=rl[db][:], scalar=wcol,
                                                                                             in1=acc[:, ch * 512:(ch + 1) * 512], op0=ALU.mult, op1=ALU.add),
                             reads=[t_rl[db], t_wi, t_acc], writes=[t_acc])
            S.op("dve", lambda e, i=i, qb=qb: e.tensor_tensor(out=acc[:, i * 512:(i + 1) * 512], in0=acc[:, i * 512:(i + 1) * 512], in1=cbt[qb][:], op=ALU.add),
                 reads=[t_acc, t_cbt[qb]], writes=[t_acc])
            if debug:
                S.dma("sp", dbg2[i, :, 0:n_i], acc[:, 0:n_i], reads=[t_acc], final=True)
            for r in range(32):
                S.op("dve", lambda e, n_i=n_i: e.max(out=mx[:], in_=acc[:, 0:n_i]), reads=[t_acc], writes=[t_mx])
                S.op("dve", lambda e, n_i=n_i: e.match_replace(out=acc[:, 0:n_i], in_to_replace=mx[:], in_values=acc[:, 0:n_i], imm_value=NEG_SEL),
                     reads=[t_acc, t_mx], writes=[t_acc])
            S.op("dve", lambda e, n_i=n_i: e.tensor_scalar(out=acc[:, 0:n_i], in0=acc[:, 0:n_i], scalar1=-2.0e38, scalar2=None, op0=ALU.is_le),
                 reads=[t_acc], writes=[t_acc])
            if debug:
                S.dma("sp", dbg[i, :, 0:n_i], acc[:, 0:n_i], reads=[t_acc], final=True)
            for j0 in range(0, 4 * (i + 1), 8):
                def ftr(eng, j0=j0):
                    for jj in range(8):
                        ins = eng.transpose(pO[:, jj * 128:(jj + 1) * 128], acc[:, (j0 + jj) * 128:(j0 + jj + 1) * 128], idf[:])
                    return ins
                nj = min(8, 4 * (i + 1) - j0)

                def ftr2(eng, j0=j0, nj=nj):
                    for jj in range(nj):
                        ins = eng.transpose(pO[:, jj * 128:(jj + 1) * 128], acc[:, (j0 + jj) * 128:(j0 + jj + 1) * 128], idf[:])
                    return ins
                S.op("pe", ftr2, reads=[t_acc, t_id], writes=[t_pO], same_ok=True)
                S.op("act", lambda e, j0=j0, nj=nj, i=i: e.activation(out=maskT[:, j0:j0 + nj, i * 128:(i + 1) * 128],
                                                                     in_=pO[:, 0:nj * 128].rearrange("p (j q) -> p j q", j=nj), func=AF.Copy),
                     reads=[t_pO], writes=[t_mT[i]])
        kT = [sb("kT%d" % i, [128, SQ], BF16) for i in range(2)]
        vt = [sb("vt%d" % i, [128, NB, 128], BF16) for i in range(2)]
        qT = [sb("qT%d" % i, [128, NQ], BF16) for i in range(2)]
        eb = [sb("eb%d" % i, [128, NS * 5 * 128], BF16) for i in range(2)]
        gt = [sb("gt%d" % i, [128, NQ], F32) for i in range(2)]
        t_kT = [Tok(), Tok()]
        t_vt = [Tok(), Tok()]
        t_qT = [Tok(), Tok()]
        t_eb = [Tok(), Tok()]
        t_gt = [Tok(), Tok()]
        PT = [sb("PT%d" % i, [128, 1024], BF16) for i in range(3)]
        t_PT = [Tok() for _ in range(3)]
        t_PTc = [Tok() for _ in range(3)]
        rd = sb("rd", [128, NQ], F32)
        t_rd = Tok()
        yt = sb("yt", [128, NQ], F32)
        t_yt = Tok()

        def head_setup(h):
            hb = h % 2
            S.dma("pool", kT[hb][:], KT[h], writes=[t_kT[hb]])
            S.dma("pool", vt[hb][:].rearrange("p j d -> p (j d)"), V[h], writes=[t_vt[hb]])
            S.dma("pool", qT[hb][:], QT[h], writes=[t_qT[hb]])
            S.dma("pool", eb[hb][:], rawb[h], writes=[t_eb[hb]])
            S.dma("sp", gt[hb][:], gT[h], writes=[t_gt[hb]])

        def head_compute(h):
            hb = h % 2
            S.op("act", lambda e: e.activation(out=eb[hb][:], in_=eb[hb][:], func=AF.Exp, bias=nc31t[:, h:h + 1]), reads=[t_eb[hb], t_c31], writes=[t_eb[hb]])
            for i in range(NS):
                jj0 = 1 if i == 0 else 0
                j0 = 4 * i - 1 + jj0
                nj = 5 - jj0

                def fcm(e, i=i, jj0=jj0, j0=j0, nj=nj):
                    ebv = eb[hb][:, (i * 5 + jj0) * 128:(i * 5 + 5) * 128].rearrange("p (j q) -> p j q", j=nj)
                    return e.tensor_tensor(out=ebv, in0=ebv, in1=maskT[:, j0:j0 + nj, i * 128:(i + 1) * 128], op=ALU.mult)
                S.op("dve", fcm, reads=[t_eb[hb], t_mT[i]], writes=[t_eb[hb]])
            S.op("act", lambda e: e.activation(out=gt[hb][:], in_=gt[hb][:], func=AF.Silu), reads=[t_gt[hb]], writes=[t_gt[hb]])

        items = [(h, j) for h in range(NH) for j in range(NB)]
        c2 = {"s": 0, "pt": 0}
        state = {}

        def chunks(c0):
            return [(c0, 512), (512, NQ)] if c0 < 512 else [(c0, NQ)]

        def imin(j):
            return max(0, (j - 3 + 3) // 4)

        def emit_qk(it):
            h, j = it
            hb = h % 2
            c0 = imin(j) * 128
            sbuf = c2["s"] % 2
            c2["s"] += 1
            state[it] = sbuf

            def f(eng):
                for a, b in chunks(c0):
                    ins = eng.matmul(pS[sbuf][:, a:b], lhsT=kT[hb][:, j * 128:(j + 1) * 128], rhs=qT[hb][:, a:b], start=True, stop=True)
                return ins
            S.op("pe", f, reads=[t_kT[hb], t_qT[hb]], writes=[t_pS[sbuf]], same_ok=True)

        def emit_rest(it):
            h, j = it
            hb = h % 2
            i0 = imin(j)
            c0 = i0 * 128
            ilast_c = min(NS - 1, (j + 1) // 4)
            cend = (ilast_c + 1) * 128
            sbuf = state.pop(it)
            pk = c2["pt"] % 3
            c2["pt"] += 1
            S.op("act", lambda e: e.activation(out=PT[pk][:, c0:NQ], in_=pS[sbuf][:, c0:NQ], func=AF.Exp, scale=SCALE, bias=c31t[:, h:h + 1]),
                 reads=[t_pS[sbuf], t_c31], writes=[t_PT[pk], t_PTc[pk]])
            if cend < NQ:
                S.op("dve", lambda e: e.tensor_tensor(out=PT[pk][:, cend:NQ], in0=PT[pk][:, cend:NQ], in1=maskT[:, j, cend:NQ], op=ALU.mult),
                     reads=[t_PT[pk]] + t_mT, writes=[t_PT[pk]])
            for i in range(i0, ilast_c + 1):
                jj = j - (4 * i - 1)
                S.op("dve", lambda e, i=i, jj=jj: e.tensor_tensor(out=PT[pk][:, i * 128:(i + 1) * 128], in0=PT[pk][:, i * 128:(i + 1) * 128],
                                                                  in1=eb[hb][:, (i * 5 + jj) * 128:(i * 5 + jj + 1) * 128], op=ALU.mult),
                     reads=[t_PTc[pk], t_eb[hb]], writes=[t_PTc[pk]])

            def f(eng):
                for a, b in chunks(c0):
                    eng.matmul(pO[:, a:b], lhsT=vt[hb][:, j, :], rhs=PT[pk][:, a:b], start=(j == 0), stop=(j == NB - 1))
                    ins = eng.matmul(pD[:, a:b], lhsT=ones_bf[:], rhs=PT[pk][:, a:b], start=(j == 0), stop=(j == NB - 1))
                return ins
            S.op("pe", f, reads=[t_PT[pk], t_PTc[pk], t_vt[hb], t_ones], writes=[t_pO, t_pD], same_ok=True)
            if j == NB - 1:
                S.op("dve", lambda e: e.reciprocal(out=rd[:], in_=pD[:]), reads=[t_pD], writes=[t_rd])
                S.op("dve", lambda e: e.tensor_tensor(out=rd[:], in0=pO[:], in1=rd[:], op=ALU.mult), reads=[t_pO, t_rd], writes=[t_rd])
                S.op("dve", lambda e: e.tensor_tensor(out=yt[:], in0=rd[:], in1=gt[hb][:], op=ALU.mult), reads=[t_rd, t_gt[hb]], writes=[t_yt])
                S.dma("sp", yTo[h], yt[:], reads=[t_yt], final=True)

        head_setup(0)
        head_compute(0)
        emit_qk(items[0])
        for n, it in enumerate(items):
            h, j = it
            if j == 0 and h + 1 < NH:
                head_setup(h + 1)
            if j == 22 and h + 1 < NH:
                head_compute(h + 1)
            if n + 1 < len(items):
                emit_qk(items[n + 1])
            emit_rest(it)
        S.emit()
    return nc


import math as _math

NCORES = 8
SEQ = 4096
BATCH = 2
_PROGS = {}


def _prog(key, fn):
    if key not in _PROGS:
        _PROGS[key] = fn()
    return _PROGS[key]


def _run(nc, in_maps):
    res = run_bass_kernel_spmd(nc, in_maps, core_ids=list(range(NCORES)))
    return res.results


def _bc(v, n=128):
    v = np.asarray(v, dtype=np.float32).reshape(-1)
    return np.ascontiguousarray(np.broadcast_to(v[None, :], (n, v.shape[0])))


def _t5_bucket(rel_):
    n = np.maximum(rel_, 0)
    nf = np.maximum(n, 1).astype(np.float32)
    large = 16 + (np.log(nf / 16) / _math.log(128 / 16) * 16).astype(np.int32)
    large = np.minimum(large, 31)
    return np.where(n < 16, n, large)


def _slot_blocks(c):
    return [4 * i + (c if i % 2 == 0 else 3 - c) for i in range(8)]


_IDENT = np.eye(128, dtype=np.float32)


def _run_P(cfg, h, yT_full, w):
    nc = _prog(("P", cfg["outproj"], cfg["ple"], cfg["nxt"], cfg["final"]), lambda: build_P(cfg))
    maps = []
    for c in range(NCORES):
        b, qq = divmod(c, 4)
        rows = slice(c * TPC, (c + 1) * TPC)
        m = {"ident": _IDENT, "h_in": np.ascontiguousarray(h[rows])}
        if cfg["outproj"]:
            m["yT"] = np.ascontiguousarray(yT_full[b][:, qq * TPC:(qq + 1) * TPC])
            m["w_out"] = w["w_out"]
        if cfg["ple"]:
            m["p"] = np.ascontiguousarray(w["p"][rows])
            m["w_gate"] = w["w_gate"]
            m["w_pp"] = w["w_pp"]
            m["ple_g"] = w["ple_g"]
        if cfg["nxt"]:
            m["norm_g"] = w["norm_g"]
            m["w_in"] = w["w_in"]
        if cfg["nxt"] == "odd":
            for k in ("qn_g", "kvn_g", "w_uq", "w_uqi", "w_ukf", "w_uvf", "ln_g", "ln_b"):
                m[k] = w[k]
        if cfg["final"]:
            m["final_g"] = w["final_g"]
        maps.append(m)
    res = _run(nc, maps)
    out = {}
    for k in res[0].keys():
        out[k] = np.concatenate([res[c][k] for c in range(NCORES)], axis=0)
    return out


def _run_E(proj, b_f, sinks, t5_table):
    nc = _prog(("E",), build_E)
    rel_band = np.arange(128)[:, None] + 128 - np.arange(256)[None, :]
    valid = (rel_band >= 0) & (rel_band < 128)
    bb = t5_table[_t5_bucket(rel_band)][..., :32].transpose(2, 0, 1)
    bm = np.where(valid[None], bb, np.float32(-30000.0)).astype(np.float32)
    sel = np.zeros((4, 4, 128), np.float32)
    for hh in range(4):
        sel[hh, hh, :] = 1
    sel = sel.reshape(4, 512)
    id4 = np.eye(4, dtype=np.float32)
    triT = np.triu(np.ones((128, 128), np.float32))
    maps = []
    for c in range(NCORES):
        b, g = divmod(c, 4)
        pr = proj[b * SEQ:(b + 1) * SEQ]
        o = 0
        q_a = pr[:, 0:2048]; k_a = pr[:, 2048:4096]; v_a = pr[:, 4096:6144]; f_a = pr[:, 6144:6160]
        g_a = pr[:, 6160:8208]; q_b = pr[:, 8208:10256]; k_b = pr[:, 10256:10512]; v_b = pr[:, 10512:10768]; g_b = pr[:, 10768:12816]
        hs = slice(4 * g * 128, (4 * g + 4) * 128)

        def hT(a):
            return np.ascontiguousarray(a.reshape(SEQ, 4, 128).transpose(1, 2, 0))
        qs = slice(8 * g * 64, (8 * g + 8) * 64)

        def hT8(a):
            return np.ascontiguousarray(a.reshape(SEQ, 8, 64).transpose(2, 1, 0))
        bmg = bm[8 * g:8 * g + 8]
        bandT = np.stack([bmg[:, :, 0:128], bmg[:, :, 128:256]], axis=1)
        bandT = np.ascontiguousarray(bandT.transpose(3, 0, 1, 2)).reshape(128, 8 * 2 * 128)
        maps.append(dict(
            qTa=hT(q_a[:, hs]), kTa=hT(k_a[:, hs]), va=np.ascontiguousarray(v_a[:, hs].reshape(32, 128, 4, 128).transpose(2, 1, 0, 3)).reshape(4, 128, 32 * 128),
            gTa=hT(g_a[:, hs]), fT=np.ascontiguousarray(f_a[:, 4 * g:4 * g + 4].T), bf=np.ascontiguousarray(b_f[4 * g:4 * g + 4].reshape(4, 1)),
            sel=sel, id4=id4, triT=triT,
            qTb=hT8(q_b[:, qs]), kTb=np.ascontiguousarray(k_b[:, g * 64:(g + 1) * 64].T), vb=np.ascontiguousarray(v_b[:, g * 64:(g + 1) * 64].reshape(32, 128, 64).transpose(1, 0, 2)).reshape(128, 32 * 64),
            gTb=hT8(g_b[:, qs]), bandT=bandT, sinkb=_bc(sinks[8 * g:8 * g + 8], 64),
        ))
    res = _run(nc, maps)
    yT_full = [np.empty((4096, SEQ), np.float32) for _ in range(BATCH)]
    for c in range(NCORES):
        b, g = divmod(c, 4)
        yT_full[b][4 * g * 128:(4 * g + 4) * 128] = res[c]["yTa"].reshape(512, SEQ)
        yb = res[c]["yTb"]
        yT_full[b][2048 + 8 * g * 64:2048 + (8 * g + 8) * 64] = yb.transpose(1, 0, 2).reshape(512, SEQ)
    return yT_full


def _run_O(po, t5_table):
    nc = _prog(("O",), build_O)
    t5_c = np.ascontiguousarray(t5_table[:, 32:])
    NHh = 32
    maps = []
    qrows_all = []
    for c8 in range(NCORES):
        b, c = divmod(c8, 4)
        tok = slice(b * SEQ, (b + 1) * SEQ)
        q = po["o_q"][tok].reshape(SEQ, NHh, 128)
        k = po["o_k"][tok].reshape(SEQ, NHh, 128)
        v = po["o_v"][tok].reshape(SEQ, NHh, 128)
        g = po["proj"][tok][:, 1632:].reshape(SEQ, NHh, 128)
        q_idx = po["o_qi"][tok].reshape(SEQ, NHh, 64)
        k_idx = po["o_ki"][tok]
        w_idx = po["o_wi"][tok]
        blks = _slot_blocks(c)
        qrows = np.concatenate([np.arange(bk * 128, (bk + 1) * 128) for bk in blks])
        qrows_all.append(qrows)
        qi = q_idx[qrows].reshape(8, 128, NHh, 64)
        qiT = np.ascontiguousarray(qi.transpose(0, 3, 2, 1)).reshape(8, 64, NHh * 128)
        kiT = np.ascontiguousarray(k_idx.T)
        wi = np.ascontiguousarray(w_idx[qrows].reshape(8, 128, NHh).transpose(1, 0, 2)).reshape(128, 8 * NHh)
        cb = np.zeros((8, 128, 512), np.float32)
        rawb = np.empty((NHh, 128, 8, 5, 128), np.float32)
        for i, bk in enumerate(blks):
            qpos = bk * 128 + np.arange(128)
            kpos = i * 512 + np.arange(512)
            cb[i] = np.where(kpos[None, :] <= qpos[:, None], np.float32(0.0), np.float32(-1.0e30))
            for jj in range(5):
                j = 4 * i - 1 + jj
                if j < 0:
                    rawb[:, :, i, jj, :] = -30000.0
                    continue
                kp = j * 128 + np.arange(128)
                r = qpos[None, :] - kp[:, None]
                vals = t5_c[_t5_bucket(r)]
                vals = np.where((r >= 0)[:, :, None], vals, np.float32(-30000.0))
                rawb[:, :, i, jj, :] = vals.transpose(2, 0, 1)
        maps.append(dict(
            ident=_IDENT, qiT=qiT, kiT=kiT, wi=wi, cb=cb,
            QT=np.ascontiguousarray(q[qrows].transpose(1, 2, 0)), KT=np.ascontiguousarray(k.transpose(1, 2, 0)),
            V=np.ascontiguousarray(v.reshape(32, 128, NHh, 128).transpose(2, 1, 0, 3)).reshape(NHh, 128, 32 * 128),
            gT=np.ascontiguousarray(g[qrows].transpose(1, 2, 0)),
            rawb=rawb.reshape(NHh, 128, 8 * 5 * 128), c31=_bc(t5_c[31]),
        ))
    res = _run(nc, maps)
    yT_full = [np.empty((4096, SEQ), np.float32) for _ in range(BATCH)]
    for c8 in range(NCORES):
        b, c = divmod(c8, 4)
        yT_full[b][:, qrows_all[c8]] = res[c8]["yTo"].reshape(4096, 1024)
    return yT_full


def kernel(x, p, t5_table, norm_g, even_w_in, even_b_f, even_sinks, even_w_out,
           odd_w_in, odd_q_norm_g, odd_kv_norm_g, odd_w_uq, odd_w_uq_idx, odd_idx_ln_g,
           odd_idx_ln_b, odd_w_uk, odd_w_uv, odd_w_out, ple_w_proj, ple_norm_g,
           ple_w_gate, final_g):
    f32 = lambda a: np.ascontiguousarray(np.asarray(a, dtype=np.float32))
    x = f32(x); p = f32(p); t5_table = f32(t5_table)
    h = x.reshape(BATCH * SEQ, 4096)
    DEPTH = 4

    def nxt_weights(i):
        j = i // 2
        w = {"norm_g": _bc(norm_g[i])}
        if i % 2 == 0:
            w["w_in"] = f32(even_w_in[j])
        else:
            w["w_in"] = f32(odd_w_in[j])
            w["qn_g"] = _bc(odd_q_norm_g[j]); w["kvn_g"] = _bc(odd_kv_norm_g[j])
            w["w_uq"] = f32(odd_w_uq[j]); w["w_uqi"] = f32(odd_w_uq_idx[j])
            w["w_ukf"] = f32(np.asarray(odd_w_uk[j]).reshape(512, 4096)); w["w_uvf"] = f32(np.asarray(odd_w_uv[j]).reshape(512, 4096))
            w["ln_g"] = _bc(odd_idx_ln_g[j]); w["ln_b"] = _bc(odd_idx_ln_b[j])
        return w

    po = _run_P(dict(outproj=False, ple=False, nxt="even", final=False), h, None, nxt_weights(0))
    out = None
    for i in range(DEPTH):
        j = i // 2
        if i % 2 == 0:
            yT_full = _run_E(po["proj"], f32(even_b_f[j]), f32(even_sinks[j]), t5_table)
            w_out = f32(even_w_out[j])
        else:
            yT_full = _run_O(po, t5_table)
            w_out = f32(odd_w_out[j])
        w = {"w_out": w_out, "p": p[i].reshape(BATCH * SEQ, 256), "w_gate": f32(ple_w_gate[i]), "w_pp": f32(ple_w_proj[i]),
             "ple_g": _bc(ple_norm_g[i])}
        if i + 1 < DEPTH:
            w.update(nxt_weights(i + 1))
            cfg = dict(outproj=True, ple=True, nxt=("even" if (i + 1) % 2 == 0 else "odd"), final=False)
        else:
            w["final_g"] = _bc(final_g)
            cfg = dict(outproj=True, ple=True, nxt=None, final=True)
        po = _run_P(cfg, h, yT_full, w)
        h = po["h_out"]
        if cfg["final"]:
            out = po["out"]
    return out.reshape(BATCH, SEQ, 4096).astype(np.float32)
```

```python
import numpy as np
import concourse.bass as bass
import concourse.mybir as mybir
from concourse.bass_utils import run_bass_kernel_spmd

F32 = mybir.dt.float32
BF16 = mybir.dt.bfloat16
AF = mybir.ActivationFunctionType
ALU = mybir.AluOpType
AX = mybir.AxisListType


class Tok:
    __slots__ = ("name", "w", "r")

    def __init__(self, name=""):
        self.name = name
        self.w = None
        self.r = []


class Op:
    __slots__ = ("eng", "fn", "deps", "dma", "signal", "count", "sem", "semval", "prev_same_sem")

    def __init__(self, eng, fn, deps, dma):
        self.eng = eng
        self.fn = fn
        self.deps = deps
        self.dma = dma
        self.signal = False
        self.count = 0
        self.sem = None
        self.semval = 0
        self.prev_same_sem = None


class Sched:
    ENGS = ("pe", "act", "dve", "pool", "sp")
    NDMA = 12

    def __init__(self, nc):
        self.nc = nc
        self.ops = {e: [] for e in self.ENGS}
        self.ndma = {e: 0 for e in self.ENGS}
        self.dma_ops = {e: [] for e in self.ENGS}
        self.final_dmas = []

    def op(self, eng, fn, reads=(), writes=(), dma=False, same_ok=False):
        deps = []
        for t in reads:
            if t.w is not None:
                deps.append(t.w)
        for t in writes:
            if t.w is not None:
                deps.append(t.w)
            deps.extend(t.r)
        seen = set()
        d2 = []
        for d in deps:
            if id(d) in seen:
                continue
            seen.add(id(d))
            if same_ok and (not d.dma) and d.eng == eng:
                continue
            d2.append(d)
        rec = Op(eng, fn, d2, dma)
        if dma:
            lst = self.dma_ops[eng]
            n = len(lst)
            if n >= self.NDMA:
                rec.prev_same_sem = lst[n - self.NDMA]
            rec.semval = 16 * (n // self.NDMA + 1)
            rec.sem = (eng, n % self.NDMA)
            lst.append(rec)
        self.ops[eng].append(rec)
        for t in reads:
            t.r.append(rec)
        for t in writes:
            t.w = rec
            t.r = []
        return rec

    def dma(self, eng, out, in_, reads=(), writes=(), final=False, **kw):
        rec = self.op(eng, lambda e: e.dma_start(out=out, in_=in_, **kw), reads, writes, dma=True)
        if final:
            self.final_dmas.append(rec)
        return rec

    def emit(self):
        nc = self.nc
        for e in self.ENGS:
            for o in self.ops[e]:
                for d in o.deps:
                    if not d.dma:
                        d.signal = True
        for e in self.ENGS:
            c = 0
            for o in self.ops[e]:
                if (not o.dma) and o.signal:
                    c += 1
                    o.count = c
        import contextlib
        with contextlib.ExitStack() as st:
            csem = {e: st.enter_context(nc.semaphore("c_" + e)) for e in ("pe", "act", "dve", "pool")}
            dsem = {}
            for e in ("act", "pool", "sp"):
                if self.dma_ops[e]:
                    for i in range(min(self.NDMA, len(self.dma_ops[e]))):
                        dsem[(e, i)] = st.enter_context(nc.semaphore("d_%s%d" % (e, i)))
            block = st.enter_context(nc.Block())
            final = self.final_dmas

            def body(ename):
                def f(eng):
                    known = {}

                    def wait(sem_key, sem, val):
                        if known.get(sem_key, 0) >= val:
                            return
                        eng.wait_ge(sem, val)
                        known[sem_key] = val

                    for o in self.ops[ename]:
                        for d in o.deps:
                            if d.dma:
                                wait(d.sem, dsem[d.sem], d.semval)
                            else:
                                wait(d.eng, csem[d.eng], d.count)
                        if o.dma:
                            if o.prev_same_sem is not None:
                                p = o.prev_same_sem
                                wait(p.sem, dsem[p.sem], p.semval)
                            ins = o.fn(eng)
                            ins.then_inc(dsem[o.sem], 16)
                        else:
                            ins = o.fn(eng)
                            if o.signal:
                                ins.then_inc(csem[ename], 1)
                    if ename == "sp":
                        for d in final:
                            wait(d.sem, dsem[d.sem], d.semval)
                return f

            block.tensor(body("pe"))
            block.scalar(body("act"))
            block.vector(body("dve"))
            block.gpsimd(body("pool"))
            block.sync(body("sp"))

import contextlib

TPC = 1024
GT = 512
D = 4096
EPS = 1e-6


def build_P(cfg):
    nc = bass.Bass("TRN2", target_bir_lowering=False)
    st = contextlib.ExitStack()

    def din(name, shape):
        return nc.dram_tensor(name, shape, F32, kind="ExternalInput").ap()

    def dout(name, shape):
        return nc.dram_tensor(name, shape, F32, kind="ExternalOutput").ap()

    def sb(name, shape, dt):
        return st.enter_context(nc.sbuf_tensor(name, shape, dt))

    def ps(name, shape, dt):
        return st.enter_context(nc.psum_tensor(name, shape, dt))

    ident = din("ident", [128, 128])
    h_in = din("h_in", [TPC, D])
    h_out = dout("h_out", [TPC, D]) if (cfg["outproj"] or cfg["ple"]) else None
    NT_H = D // 512
    if cfg["outproj"]:
        yT = din("yT", [D, TPC])
        w_out = din("w_out", [D, D])
    if cfg["ple"]:
        p_in = din("p", [TPC, 256])
        w_gate = din("w_gate", [D, D])
        w_pp = din("w_pp", [256, D])
        ple_g = din("ple_g", [128, D])
    nxt = cfg["nxt"]
    if nxt:
        NIN = 12816 if nxt == "even" else 5728
        norm_g = din("norm_g", [128, D])
        w_in = din("w_in", [D, NIN])
        proj = dout("proj", [TPC, NIN])
    if nxt == "odd":
        qn_g = din("qn_g", [128, 1024])
        kvn_g = din("kvn_g", [128, 512])
        w_uq = din("w_uq", [1024, 4096])
        w_uqi = din("w_uqi", [1024, 2048])
        w_ukf = din("w_ukf", [512, 4096])
        w_uvf = din("w_uvf", [512, 4096])
        ln_g = din("ln_g", [128, 64])
        ln_b = din("ln_b", [128, 64])
        o_q = dout("o_q", [TPC, 4096])
        o_qi = dout("o_qi", [TPC, 2048])
        o_k = dout("o_k", [TPC, 4096])
        o_v = dout("o_v", [TPC, 4096])
        o_ki = dout("o_ki", [TPC, 64])
        o_wi = dout("o_wi", [TPC, 32])
    if cfg["final"]:
        final_g = din("final_g", [128, D])
        out = dout("out", [TPC, D])

    with st:
        S = Sched(nc)
        idb = sb("idb", [128, 128], BF16)
        t_idb = Tok()
        S.dma("pool", idb[:], ident, writes=[t_idb])
        xrow = [sb("xrow%d" % i, [128, D], F32) for i in range(2)]
        t_xrow = [Tok(), Tok()]
        hnb = [sb("hnb%d" % i, [128, D], BF16) for i in range(2)]
        t_hnb = [Tok(), Tok()]
        gbc = sb("gbc", [128, D], F32)
        t_gbc = Tok()
        xT = sb("xT", [128, 32, GT], BF16)
        t_xT = [Tok() for _ in range(GT // 128)]
        wt = [sb("wt%d" % i, [128, 32, 512], BF16) for i in range(2)]
        t_wt = [Tok(), Tok()]
        ot = [sb("ot%d" % i, [128, 512], F32) for i in range(3)]
        t_ot = [Tok() for _ in range(3)]
        hs = [sb("hs%d" % i, [128, 512], F32) for i in range(3)]
        t_hs = [Tok() for _ in range(3)]
        ss = sb("ss", [128, 8], F32)
        t_ss = Tok()
        epsb = sb("epsb", [128, 1], F32)
        t_eps = Tok()
        S.op("dve", lambda e: e.memset(epsb[:], EPS), writes=[t_eps])
        ptr = [ps("ptr%d" % i, [128, 512], BF16) for i in range(2)]
        t_ptr = [Tok(), Tok()]
        pmm = [ps("pmm%d" % i, [128, 512], F32) for i in range(3)]
        t_pmm = [Tok() for _ in range(3)]
        cnt = {"tr": 0, "mm": 0, "w": 0, "row": 0, "ot": 0, "hs": 0}
        if cfg["ple"]:
            pu = [ps("pu%d" % i, [128, 512], F32) for i in range(2)]
            t_pu = [Tok(), Tok()]
            prow = sb("prow", [128, 256], BF16)
            t_prow = Tok()
            pT = sb("pT", [128, 2, GT], BF16)
            t_pT = [Tok() for _ in range(GT // 128)]
            wpt = [sb("wpt%d" % i, [128, 2, 512], BF16) for i in range(2)]
            t_wpt = [Tok(), Tok()]
            sg = [sb("sg%d" % i, [128, 512], F32) for i in range(2)]
            t_sg = [Tok(), Tok()]
        if nxt == "odd":
            cT = sb("cT", [128, 12, GT], BF16)
            t_cT = [Tok() for _ in range(GT // 128)]
            sm = sb("sm", [128, 256], F32)
            t_sm = Tok()
            lg = sb("lg", [128, 64], F32)
            lb = sb("lb", [128, 64], F32)
            t_lgb = Tok()
            S.dma("sp", lg[:], ln_g, writes=[t_lgb])
            S.dma("sp", lb[:], ln_b, writes=[t_lgb])

        def transposes(src_bf, t_src, nch, dstT, t_dst, tl, ch0=0):
            for g0 in range(0, nch, 4):
                n = min(4, nch - g0)
                pb = cnt["tr"] % 2
                cnt["tr"] += 1

                def f(eng, g0=g0, n=n, pb=pb):
                    for j in range(n):
                        c = g0 + j
                        ins = eng.transpose(ptr[pb][:, j * 128:(j + 1) * 128], src_bf[:, c * 128:(c + 1) * 128], idb[:])
                    return ins
                S.op("pe", f, reads=[t_src, t_idb], writes=[t_ptr[pb]], same_ok=True)
                en = "act" if (cnt["tr"] % 2) else "dve"

                def f2(eng, g0=g0, n=n, pb=pb, en=en):
                    o = dstT[:, ch0 + g0:ch0 + g0 + n, tl * 128:(tl + 1) * 128]
                    i = ptr[pb][:, 0:n * 128].rearrange("p (j t) -> p j t", j=n)
                    if en == "act":
                        return eng.activation(out=o, in_=i, func=AF.Copy)
                    return eng.tensor_copy(out=o, in_=i)
                S.op(en, f2, reads=[t_ptr[pb]], writes=[t_dst])

        def load_gbc(src, width=D):
            S.dma("sp", gbc[:, 0:width], src, writes=[t_gbc])

        def norm_rows(src_rows, src_toks, width, dstT, t_dst, tl, ch0=0):
            b = cnt["row"] % 2
            cnt["row"] += 1
            S.dma("sp", xrow[b][:, 0:width], src_rows, reads=src_toks, writes=[t_xrow[b]])
            col = cnt["row"] % 8
            S.op("act", lambda e: e.activation(out=hnb[b][:, 0:width], in_=xrow[b][:, 0:width], func=AF.Square,
                                              scale=float(width) ** -0.5, accum_out=ss[:, col:col + 1]),
                 reads=[t_xrow[b]], writes=[t_hnb[b], t_ss])
            S.op("act", lambda e: e.activation(out=ss[:, col:col + 1], in_=ss[:, col:col + 1], func=AF.Sqrt, bias=epsb[:]), reads=[t_ss, t_eps], writes=[t_ss])
            S.op("dve", lambda e: e.reciprocal(out=ss[:, col:col + 1], in_=ss[:, col:col + 1]), reads=[t_ss], writes=[t_ss])
            S.op("dve", lambda e: e.scalar_tensor_tensor(out=hnb[b][:, 0:width], in0=xrow[b][:, 0:width], scalar=ss[:, col:col + 1],
                                                        in1=gbc[:, 0:width], op0=ALU.mult, op1=ALU.mult),
                 reads=[t_xrow[b], t_ss, t_gbc], writes=[t_hnb[b]])
            transposes(hnb[b], t_hnb[b], width // 128, dstT, t_dst, tl, ch0)

        def final_rows(src, toks, dst):
            b = cnt["row"] % 2
            cnt["row"] += 1
            col = cnt["row"] % 8
            S.dma("sp", xrow[b][:], src, reads=toks, writes=[t_xrow[b]])
            S.op("act", lambda e: e.activation(out=hnb[b][:], in_=xrow[b][:], func=AF.Square, scale=float(D) ** -0.5,
                                              accum_out=ss[:, col:col + 1]), reads=[t_xrow[b]], writes=[t_hnb[b], t_ss])
            S.op("act", lambda e: e.activation(out=ss[:, col:col + 1], in_=ss[:, col:col + 1], func=AF.Sqrt, bias=epsb[:]), reads=[t_ss, t_eps], writes=[t_ss])
            S.op("dve", lambda e: e.reciprocal(out=ss[:, col:col + 1], in_=ss[:, col:col + 1]), reads=[t_ss], writes=[t_ss])
            S.op("dve", lambda e: e.scalar_tensor_tensor(out=xrow[b][:], in0=xrow[b][:], scalar=ss[:, col:col + 1],
                                                        in1=gbc[:], op0=ALU.mult, op1=ALU.mult),
                 reads=[t_xrow[b], t_ss, t_gbc], writes=[t_xrow[b]])
            S.dma("sp", dst, xrow[b][:], reads=[t_xrow[b]], final=True)

        def linear(xTbuf, t_x, kch0, nk, W, ncols, epilogue, ntl=GT // 128, extra=None):
            for n0 in range(0, ncols, 512):
                nw = min(512, ncols - n0)
                wb = cnt["w"] % 2
                cnt["w"] += 1
                S.dma("pool", wt[wb][:, 0:nk, 0:nw], W[:, n0:n0 + nw].rearrange("(c p) n -> p c n", p=128), writes=[t_wt[wb]])
                ex = extra(n0, nw) if extra else None
                for tl in range(ntl):
                    pb = cnt["mm"] % 3
                    cnt["mm"] += 1

                    def f(eng, wb=wb, tl=tl, pb=pb, nw=nw):
                        for c in range(nk):
                            ins = eng.matmul(pmm[pb][:, 0:nw], lhsT=xTbuf[:, kch0 + c, tl * 128:(tl + 1) * 128],
                                             rhs=wt[wb][:, c, 0:nw], start=(c == 0), stop=(c == nk - 1))
                        return ins
                    S.op("pe", f, reads=[t_x[tl], t_wt[wb]], writes=[t_pmm[pb]], same_ok=True)
                    epilogue(tl, n0, nw, pmm[pb], t_pmm[pb], ex)

        def store_epi(dst, t0):
            def epi(tl, n0, nw, pm, t_pm, ex):
                k = cnt["ot"] % 3
                cnt["ot"] += 1
                en = "act" if k % 2 else "dve"
                if en == "act":
                    S.op("act", lambda e: e.activation(out=ot[k][:, 0:nw], in_=pm[:, 0:nw], func=AF.Copy), reads=[t_pm], writes=[t_ot[k]])
                else:
                    S.op("dve", lambda e: e.tensor_copy(out=ot[k][:, 0:nw], in_=pm[:, 0:nw]), reads=[t_pm], writes=[t_ot[k]])
                S.dma("sp", dst[t0 + tl * 128:t0 + (tl + 1) * 128, n0:n0 + nw], ot[k][:, 0:nw], reads=[t_ot[k]], final=True)
            return epi

        for grp in range(TPC // GT):
            t0 = grp * GT
            t_h = [[Tok() for _ in range(NT_H)] for _ in range(GT // 128)]
            hsrc = h_in
            if cfg["outproj"]:
                S.dma("pool", xT[:], yT[:, t0:t0 + GT].rearrange("(c p) t -> p c t", p=128), writes=t_xT)

                def epi1(tl, n0, nw, pm, t_pm, ex):
                    k = cnt["hs"] % 3
                    cnt["hs"] += 1
                    rows = slice(t0 + tl * 128, t0 + (tl + 1) * 128)
                    S.dma("sp", hs[k][:], h_in[rows, n0:n0 + nw], writes=[t_hs[k]])
                    S.op("dve", lambda e: e.tensor_tensor(out=hs[k][:], in0=pm[:], in1=hs[k][:], op=ALU.add), reads=[t_pm, t_hs[k]], writes=[t_hs[k]])
                    S.dma("sp", h_out[rows, n0:n0 + nw], hs[k][:], reads=[t_hs[k]], writes=[t_h[tl][n0 // 512]], final=True)
                linear(xT, t_xT, 0, 32, w_out, D, epi1)
                hsrc = h_out
            if cfg["ple"]:
                load_gbc(ple_g)
                for tl in range(GT // 128):
                    rows = slice(t0 + tl * 128, t0 + (tl + 1) * 128)
                    norm_rows(hsrc[rows, :], t_h[tl], D, xT, t_xT[tl], tl)
                    S.dma("pool", prow[:], p_in[rows, :], writes=[t_prow])
                    transposes(prow, t_prow, 2, pT, t_pT[tl], tl)

                def extra2(n0, nw):
                    wb = (n0 // 512) % 2
                    S.dma("pool", wpt[wb][:], w_pp[:, n0:n0 + nw].rearrange("(c p) n -> p c n", p=128), writes=[t_wpt[wb]])
                    return wb

                def epi2(tl, n0, nw, pm, t_pm, wb):
                    ub = cnt["hs"] % 2
                    k = cnt["hs"] % 3
                    cnt["hs"] += 1
                    rows = slice(t0 + tl * 128, t0 + (tl + 1) * 128)

                    def fu(eng):
                        for c in range(2):
                            ins = eng.matmul(pu[ub][:], lhsT=pT[:, c, tl * 128:(tl + 1) * 128], rhs=wpt[wb][:, c, :], start=(c == 0), stop=(c == 1))
                        return ins
                    S.op("pe", fu, reads=[t_pT[tl], t_wpt[wb]], writes=[t_pu[ub]], same_ok=True)
                    S.dma("sp", hs[k][:], hsrc[rows, n0:n0 + nw], reads=[t_h[tl][n0 // 512]], writes=[t_hs[k]])
                    S.op("act", lambda e: e.activation(out=sg[ub][:], in_=pm[:], func=AF.Sigmoid), reads=[t_pm], writes=[t_sg[ub]])
                    S.op("dve", lambda e: e.tensor_tensor(out=sg[ub][:], in0=pu[ub][:], in1=sg[ub][:], op=ALU.mult), reads=[t_pu[ub], t_sg[ub]], writes=[t_sg[ub]])
                    S.op("dve", lambda e: e.tensor_tensor(out=hs[k][:], in0=sg[ub][:], in1=hs[k][:], op=ALU.add), reads=[t_sg[ub], t_hs[k]], writes=[t_hs[k]])
                    S.dma("sp", h_out[rows, n0:n0 + nw], hs[k][:], reads=[t_hs[k]], writes=[t_h[tl][n0 // 512]], final=True)
                linear(xT, t_xT, 0, 32, w_gate, D, epi2, extra=extra2)
                hsrc = h_out
            if nxt:
                load_gbc(norm_g)
                for tl in range(GT // 128):
                    rows = slice(t0 + tl * 128, t0 + (tl + 1) * 128)
                    norm_rows(hsrc[rows, :], t_h[tl], D, xT, t_xT[tl], tl)
                if nxt == "even":
                    linear(xT, t_xT, 0, 32, w_in, NIN, store_epi(proj, t0))
                else:
                    t_pr = [[Tok() for _ in range(4)] for _ in range(GT // 128)]

                    def epi3(tl, n0, nw, pm, t_pm, ex):
                        k = cnt["ot"] % 3
                        cnt["ot"] += 1
                        S.op("dve", lambda e: e.tensor_copy(out=ot[k][:, 0:nw], in_=pm[:, 0:nw]), reads=[t_pm], writes=[t_ot[k]])
                        wr = [t_pr[tl][n0 // 512]] if n0 < 2048 else []
                        S.dma("sp", proj[t0 + tl * 128:t0 + (tl + 1) * 128, n0:n0 + nw], ot[k][:, 0:nw], reads=[t_ot[k]], writes=wr, final=True)
                    linear(xT, t_xT, 0, 32, w_in, NIN, epi3)
                    load_gbc(qn_g, 1024)
                    for tl in range(GT // 128):
                        rows = slice(t0 + tl * 128, t0 + (tl + 1) * 128)
                        norm_rows(proj[rows, 0:1024], t_pr[tl][0:2], 1024, cT, t_cT[tl], tl, 0)
                    load_gbc(kvn_g, 512)
                    for tl in range(GT // 128):
                        rows = slice(t0 + tl * 128, t0 + (tl + 1) * 128)
                        norm_rows(proj[rows, 1024:1536], t_pr[tl][2:3], 512, cT, t_cT[tl], tl, 8)
                        S.dma("sp", sm[:, 0:96], proj[rows, 1536:1632], reads=t_pr[tl][3:4], writes=[t_sm])
                        S.op("dve", lambda e: e.reduce_sum(out=sm[:, 100:101], in_=sm[:, 0:64], axis=AX.X), reads=[t_sm], writes=[t_sm])
                        S.op("dve", lambda e: e.tensor_scalar(out=sm[:, 100:101], in0=sm[:, 100:101], scalar1=-1.0 / 64, scalar2=None, op0=ALU.mult), reads=[t_sm], writes=[t_sm])
                        S.op("dve", lambda e: e.tensor_scalar(out=sm[:, 0:64], in0=sm[:, 0:64], scalar1=sm[:, 100:101], scalar2=None, op0=ALU.add), reads=[t_sm], writes=[t_sm])
                        S.op("act", lambda e: e.activation(out=sm[:, 128:192], in_=sm[:, 0:64], func=AF.Square, scale=0.125, accum_out=sm[:, 101:102]), reads=[t_sm], writes=[t_sm])
                        S.op("act", lambda e: e.activation(out=sm[:, 101:102], in_=sm[:, 101:102], func=AF.Sqrt, bias=epsb[:]), reads=[t_sm, t_eps], writes=[t_sm])
                        S.op("dve", lambda e: e.reciprocal(out=sm[:, 101:102], in_=sm[:, 101:102]), reads=[t_sm], writes=[t_sm])
                        S.op("dve", lambda e: e.scalar_tensor_tensor(out=sm[:, 0:64], in0=sm[:, 0:64], scalar=sm[:, 101:102], in1=lg[:], op0=ALU.mult, op1=ALU.mult), reads=[t_sm, t_lgb], writes=[t_sm])
                        S.op("dve", lambda e: e.tensor_tensor(out=sm[:, 0:64], in0=sm[:, 0:64], in1=lb[:], op=ALU.add), reads=[t_sm, t_lgb], writes=[t_sm])
                        S.op("dve", lambda e: e.tensor_scalar(out=sm[:, 64:96], in0=sm[:, 64:96], scalar1=float(32 ** -0.5 * 64 ** -0.5), scalar2=None, op0=ALU.mult), reads=[t_sm], writes=[t_sm])
                        S.dma("sp", o_ki[rows, :], sm[:, 0:64], reads=[t_sm], final=True)
                        S.dma("sp", o_wi[rows, :], sm[:, 64:96], reads=[t_sm], final=True)
                    linear(cT, t_cT, 0, 8, w_uq, 4096, store_epi(o_q, t0))
                    linear(cT, t_cT, 0, 8, w_uqi, 2048, store_epi(o_qi, t0))
                    linear(cT, t_cT, 8, 4, w_ukf, 4096, store_epi(o_k, t0))
                    linear(cT, t_cT, 8, 4, w_uvf, 4096, store_epi(o_v, t0))
            if cfg["final"]:
                load_gbc(final_g)
                for tl in range(GT // 128):
                    rows = slice(t0 + tl * 128, t0 + (tl + 1) * 128)
                    final_rows(hsrc[rows, :], t_h[tl], out[rows, :])
        S.emit()
    return nc

import contextlib

SQ = 4096
NB = SQ // 128
R128 = float(128 ** 0.5)


def build_E():
    nc = bass.Bass("TRN2", target_bir_lowering=False)
    st = contextlib.ExitStack()

    def din(name, shape):
        return nc.dram_tensor(name, shape, F32, kind="ExternalInput").ap()

    def dout(name, shape):
        return nc.dram_tensor(name, shape, F32, kind="ExternalOutput").ap()

    def sb(name, shape, dt):
        return st.enter_context(nc.sbuf_tensor(name, shape, dt))

    def ps(name, shape, dt):
        return st.enter_context(nc.psum_tensor(name, shape, dt))

    qTa = din("qTa", [4, 128, SQ])
    kTa = din("kTa", [4, 128, SQ])
    va = din("va", [4, 128, NB * 128])
    gTa = din("gTa", [4, 128, SQ])
    fT = din("fT", [4, SQ])
    bf = din("bf", [4, 1])
    sel = din("sel", [4, 4 * 128])
    id4 = din("id4", [4, 4])
    triT = din("triT", [128, 128])
    yTa = dout("yTa", [4, 128, SQ])
    qTb = din("qTb", [64, 8, SQ])
    kTb = din("kTb", [64, SQ])
    vb = din("vb", [128, NB * 64])
    gTb = din("gTb", [64, 8, SQ])
    bandT = din("bandT", [128, 8 * 2 * 128])
    sinkb = din("sinkb", [64, 8])
    yTb = dout("yTb", [64, 8, SQ])

    with st:
        S = Sched(nc)
        ones_bf = sb("ones_bf", [128, 128], BF16)
        t_ones = Tok()
        S.op("dve", lambda e: e.memset(ones_bf[:], 1.0), writes=[t_ones])
        tri = sb("tri", [128, 128], BF16)
        t_tri = Tok()
        S.dma("pool", tri[:], triT, writes=[t_tri])
        selt = sb("selt", [4, 512], F32)
        id4t = sb("id4t", [4, 4], F32)
        t_sel = Tok()
        S.dma("sp", selt[:], sel, writes=[t_sel])
        S.dma("sp", id4t[:], id4, writes=[t_sel])
        f4 = sb("f4", [4, SQ], F32)
        e4 = sb("e4", [4, SQ], F32)
        ones4 = sb("ones4", [4, 512], F32)
        nb = sb("nb", [4, 1], F32)
        t_f4 = Tok()
        t_e4 = Tok()
        t_nb = Tok()
        t_o4 = Tok()
        S.dma("sp", f4[:], fT, writes=[t_f4])
        S.dma("sp", nb[:], bf, writes=[t_nb])
        S.op("dve", lambda e: e.memset(ones4[:], 1.0), writes=[t_o4])
        S.op("dve", lambda e: e.tensor_scalar(out=nb[:], in0=nb[:], scalar1=-1.0, scalar2=None, op0=ALU.mult), reads=[t_nb], writes=[t_nb])
        S.op("act", lambda e: e.activation(out=e4[:], in_=f4[:], func=AF.Exp, scale=-1.0, bias=nb[:, 0:1]), reads=[t_f4, t_nb], writes=[t_e4])
        S.op("act", lambda e: e.activation(out=f4[:], in_=e4[:], func=AF.Ln, scale=1.0, bias=1.0), reads=[t_e4], writes=[t_f4])
        for c in range(SQ // 512):
            S.op("dve", lambda e, c=c: e.tensor_tensor_scan(out=e4[:, c * 512:(c + 1) * 512], data0=ones4[:], data1=f4[:, c * 512:(c + 1) * 512],
                                                          initial=(0.0 if c == 0 else e4[:, c * 512 - 1:c * 512]), op0=ALU.mult, op1=ALU.add),
                 reads=[t_f4, t_o4, t_e4], writes=[t_e4])
        csp = e4
        t_csp = t_e4
        pst = ps("pst", [128, 512], F32)
        t_pst = Tok()
        acol = sb("acol", [128, NB * 4], F32)
        t_acol = Tok()

        def f_tr(eng):
            for blk in range(NB):
                ins = eng.transpose(pst[:, blk * 4:(blk + 1) * 4], csp[0:4, blk * 128:(blk + 1) * 128], id4t[0:4, 0:4])
            return ins
        S.op("pe", f_tr, reads=[t_csp, t_sel], writes=[t_pst], same_ok=True)
        S.op("act", lambda e: e.activation(out=acol[:], in_=pst[:, 0:NB * 4], func=AF.Copy, scale=R128), reads=[t_pst], writes=[t_acol])

        negcb = sb("negcb", [128, SQ], F32)
        t_negcb = Tok()
        kT = [sb("kT%d" % i, [128, SQ], BF16) for i in range(2)]
        qT = [sb("qT%d" % i, [128, SQ], BF16) for i in range(2)]
        vt = [sb("vt%d" % i, [128, NB, 128], BF16) for i in range(2)]
        t_kT = [Tok(), Tok()]
        t_qT = [Tok(), Tok()]
        t_vt = [Tok(), Tok()]
        pS = [ps("pS%d" % i, [128, 512], F32) for i in range(3)]
        t_pS = [Tok(), Tok(), Tok()]
        pO = [ps("pO%d" % i, [128, 512], F32) for i in range(2)]
        t_pO = [Tok(), Tok()]
        pD = [ps("pD%d" % i, [128, 512], F32) for i in range(2)]
        t_pD = [Tok(), Tok()]
        tmp = [sb("tmp%d" % i, [128, 512], F32) for i in range(3)]
        t_tmp = [Tok(), Tok(), Tok()]
        PT = [sb("PT%d" % i, [128, 512], BF16) for i in range(4)]
        t_PT = [Tok() for _ in range(4)]
        g32 = [sb("g32_%d" % i, [128, 512], F32) for i in range(2)]
        t_g32 = [Tok(), Tok()]
        rd = [sb("rd%d" % i, [128, 512], F32) for i in range(2)]
        t_rd = [Tok(), Tok()]
        yt = [sb("yt%d" % i, [128, 512], F32) for i in range(2)]
        t_yt = [Tok(), Tok()]

        def head_setup(h):
            hb = h % 2
            S.dma("pool", kT[hb][:], kTa[h], writes=[t_kT[hb]])
            S.dma("pool", qT[hb][:], qTa[h], writes=[t_qT[hb]])
            S.dma("pool", vt[hb][:].rearrange("p j d -> p (j d)"), va[h], writes=[t_vt[hb]])

        def negcb_setup(h):
            for c in range(SQ // 512):
                S.op("pe", lambda e, c=c: e.matmul(pst[:], lhsT=selt[0:4, h * 128:(h + 1) * 128], rhs=csp[0:4, c * 512:(c + 1) * 512], start=True, stop=True),
                     reads=[t_csp, t_sel], writes=[t_pst])
                S.op("act", lambda e, c=c: e.activation(out=negcb[:, c * 512:(c + 1) * 512], in_=pst[:], func=AF.Copy, scale=-R128),
                     reads=[t_pst], writes=[t_negcb])

        items = []
        for h in range(4):
            for Q in range(SQ // 512):
                for j in range(4 * Q + 4):
                    items.append((h, Q, j))
        cnt = {"s": 0, "pt": 0, "fin": 0}
        state = {}

        def emit_qk(it):
            h, Q, j = it
            hb = h % 2
            c0 = 0 if j < 4 * Q else (j - 4 * Q) * 128
            sbuf = cnt["s"] % 3
            cnt["s"] += 1
            state[it] = sbuf
            S.op("pe", lambda e: e.matmul(pS[sbuf][:, c0:512], lhsT=kT[hb][:, j * 128:(j + 1) * 128],
                                          rhs=qT[hb][:, Q * 512 + c0:(Q + 1) * 512], start=True, stop=True),
                 reads=[t_kT[hb], t_qT[hb]], writes=[t_pS[sbuf]])

        def emit_rest(it):
            h, Q, j = it
            hb = h % 2
            c0 = 0 if j < 4 * Q else (j - 4 * Q) * 128
            sbuf = state.pop(it)
            ob = (h * 8 + Q) % 2
            pk = cnt["pt"] % 4
            cnt["pt"] += 1
            last = (j == 4 * Q + 3)
            S.op("dve", lambda e: e.scalar_tensor_tensor(out=tmp[sbuf][:, c0:512], in0=pS[sbuf][:, c0:512], scalar=acol[:, j * 4 + h:j * 4 + h + 1],
                                                        in1=negcb[:, Q * 512 + c0:(Q + 1) * 512], op0=ALU.add, op1=ALU.add),
                 reads=[t_pS[sbuf], t_acol, t_negcb], writes=[t_tmp[sbuf]])
            S.op("act", lambda e: e.activation(out=PT[pk][:, c0:512], in_=tmp[sbuf][:, c0:512], func=AF.Exp, scale=1.0 / R128),
                 reads=[t_tmp[sbuf]], writes=[t_PT[pk]])
            if j >= 4 * Q:
                S.op("dve", lambda e: e.tensor_tensor(out=PT[pk][:, c0:c0 + 128], in0=PT[pk][:, c0:c0 + 128], in1=tri[:], op=ALU.mult),
                     reads=[t_PT[pk], t_tri], writes=[t_PT[pk]])

            def f(eng):
                eng.matmul(pO[ob][:, c0:512], lhsT=vt[hb][:, j, :], rhs=PT[pk][:, c0:512], start=(j == 0), stop=last)
                return eng.matmul(pD[ob][:, c0:512], lhsT=ones_bf[:], rhs=PT[pk][:, c0:512], start=(j == 0), stop=last)
            S.op("pe", f, reads=[t_PT[pk], t_vt[hb], t_ones], writes=[t_pO[ob], t_pD[ob]], same_ok=True)
            if last:
                k = cnt["fin"] % 2
                cnt["fin"] += 1
                S.dma("sp", g32[k][:], gTa[h, :, Q * 512:(Q + 1) * 512], writes=[t_g32[k]])
                S.op("act", lambda e: e.activation(out=g32[k][:], in_=g32[k][:], func=AF.Silu), reads=[t_g32[k]], writes=[t_g32[k]])
                S.op("dve", lambda e: e.reciprocal(out=rd[k][:], in_=pD[ob][:]), reads=[t_pD[ob]], writes=[t_rd[k]])
                S.op("dve", lambda e: e.tensor_tensor(out=rd[k][:], in0=pO[ob][:], in1=rd[k][:], op=ALU.mult), reads=[t_pO[ob], t_rd[k]], writes=[t_rd[k]])
                S.op("dve", lambda e: e.tensor_tensor(out=yt[k][:], in0=rd[k][:], in1=g32[k][:], op=ALU.mult), reads=[t_rd[k], t_g32[k]], writes=[t_yt[k]])
                S.dma("sp", yTa[h, :, Q * 512:(Q + 1) * 512], yt[k][:], reads=[t_yt[k]], final=True)

        head_setup(0)
        negcb_setup(0)
        emit_qk(items[0])
        emit_qk(items[1])
        for n, it in enumerate(items):
            h, Q, j = it
            if Q == 0 and j == 0 and h + 1 < 4:
                head_setup(h + 1)
            nx = items[n + 1] if n + 1 < len(items) else None
            nx2 = items[n + 2] if n + 2 < len(items) else None
            if nx2 is not None:
                emit_qk(nx2)
            emit_rest(it)
            if nx is not None and nx[0] != h:
                negcb_setup(nx[0])

        EB = sb("EB", [128, 2048], BF16)
        t_EB = Tok()
        S.dma("sp", negcb[:, 0:2048], bandT, writes=[t_negcb])
        S.op("act", lambda e: e.activation(out=EB[:], in_=negcb[:, 0:2048], func=AF.Exp), reads=[t_negcb], writes=[t_EB])
        sk8 = sb("sk8", [64, 8], F32)
        SK = sb("SK", [64, 8 * 128], F32)
        t_SK = Tok()
        S.dma("sp", sk8[:], sinkb, writes=[t_SK])
        S.op("act", lambda e: e.activation(out=sk8[:], in_=sk8[:], func=AF.Exp), reads=[t_SK], writes=[t_SK])
        S.op("dve", lambda e: e.memset(SK[:], 0.0), writes=[t_SK])
        for hh in range(8):
            S.op("dve", lambda e, hh=hh: e.tensor_scalar(out=SK[:, hh * 128:(hh + 1) * 128], in0=SK[:, hh * 128:(hh + 1) * 128],
                                                        scalar1=sk8[:, hh:hh + 1], scalar2=None, op0=ALU.add), reads=[t_SK], writes=[t_SK])
        kTs = sb("kTs", [64, SQ], BF16)
        vs = sb("vs", [128, NB, 64], BF16)
        t_kvs = Tok()
        S.dma("pool", kTs[:], kTb, writes=[t_kvs])
        S.dma("pool", vs[:].rearrange("p j d -> p (j d)"), vb, writes=[t_kvs])
        qTs = sb("qTs", [64, 4, SQ], BF16)
        t_qTs = Tok()
        gs = [sb("gs%d" % i, [64, 4, 512], F32) for i in range(2)]
        t_gs = [Tok(), Tok()]
        Es = [sb("Es%d" % i, [128, 512], BF16) for i in range(3)]
        t_Es = [Tok(), Tok(), Tok()]
        den = [sb("den%d" % i, [64, 512], F32) for i in range(2)]
        t_den = [Tok(), Tok()]
        ys = [sb("ys%d" % i, [64, 4, 128], F32) for i in range(2)]
        t_ys = [Tok(), Tok()]
        c2 = {"s": 0}
        sw_items = []
        for hg in range(2):
            for i in range(NB):
                parts = ([(i - 1, 0)] if i > 0 else []) + [(i, 1)]
                for pi, (kb, which) in enumerate(parts):
                    sw_items.append((hg, i, pi, kb, which, len(parts)))
        sw_state = {}

        def gate_dma(hg, ci):
            gk = (hg * 8 + ci) % 2
            S.dma("sp", gs[gk][:], gTb[:, hg * 4:(hg + 1) * 4, ci * 512:(ci + 1) * 512], writes=[t_gs[gk]])

        def gate_silu(hg, ci):
            gk = (hg * 8 + ci) % 2
            S.op("act", lambda e: e.activation(out=gs[gk][:], in_=gs[gk][:], func=AF.Silu), reads=[t_gs[gk]], writes=[t_gs[gk]])

        def sw_qk(it):
            hg, i, pi, kb, which, np_ = it
            sbuf = c2["s"] % 3
            c2["s"] += 1
            sw_state[it] = sbuf

            def fqk(eng):
                for hh in range(4):
                    ins = eng.matmul(pS[sbuf][:, hh * 128:(hh + 1) * 128], lhsT=kTs[:, kb * 128:(kb + 1) * 128],
                                     rhs=qTs[:, hh, i * 128:(i + 1) * 128], start=True, stop=True)
                return ins
            S.op("pe", fqk, reads=[t_kvs, t_qTs], writes=[t_pS[sbuf]], same_ok=True)

        def sw_rest(it):
            hg, i, pi, kb, which, np_ = it
            sbuf = sw_state.pop(it)
            ob = (hg * NB + i) % 2
            S.op("act", lambda e: e.activation(out=Es[sbuf][:], in_=pS[sbuf][:], func=AF.Exp, scale=0.125),
                 reads=[t_pS[sbuf]], writes=[t_Es[sbuf]])

            def fmul(eng):
                ebv = EB[:].rearrange("p (h w q) -> p h w q", h=8, w=2)[:, hg * 4:(hg + 1) * 4, which, :]
                return eng.tensor_tensor(out=Es[sbuf][:].rearrange("p (h q) -> p h q", h=4), in0=Es[sbuf][:].rearrange("p (h q) -> p h q", h=4),
                                         in1=ebv, op=ALU.mult)
            S.op("dve", fmul, reads=[t_Es[sbuf], t_EB], writes=[t_Es[sbuf]])

            def fpv(eng):
                eng.matmul(pO[ob][0:64, :], lhsT=vs[:, kb, :], rhs=Es[sbuf][:], start=(pi == 0), stop=(pi == np_ - 1))
                return eng.matmul(pD[ob][0:64, :], lhsT=ones_bf[:, 0:64], rhs=Es[sbuf][:], start=(pi == 0), stop=(pi == np_ - 1))
            S.op("pe", fpv, reads=[t_Es[sbuf], t_kvs, t_ones], writes=[t_pO[ob], t_pD[ob]], same_ok=True)
            if pi == np_ - 1:
                k = ob
                gk = (hg * 8 + i // 4) % 2
                S.op("dve", lambda e: e.tensor_tensor(out=den[k][:], in0=pD[k][0:64, :], in1=SK[:, hg * 512:(hg + 1) * 512], op=ALU.add),
                     reads=[t_pD[k], t_SK], writes=[t_den[k]])
                S.op("dve", lambda e: e.reciprocal(out=den[k][:], in_=den[k][:]), reads=[t_den[k]], writes=[t_den[k]])
                S.op("dve", lambda e: e.tensor_tensor(out=den[k][:], in0=pO[k][0:64, :], in1=den[k][:], op=ALU.mult),
                     reads=[t_pO[k], t_den[k]], writes=[t_den[k]])
                S.op("dve", lambda e: e.tensor_tensor(out=ys[k][:], in0=den[k][:].rearrange("p (h q) -> p h q", h=4),
                                                     in1=gs[gk][:, :, (i % 4) * 128:(i % 4 + 1) * 128], op=ALU.mult),
                     reads=[t_den[k], t_gs[gk]], writes=[t_ys[k]])
                S.dma("sp", yTb[:, hg * 4:(hg + 1) * 4, i * 128:(i + 1) * 128], ys[k][:], reads=[t_ys[k]], final=True)

        LA2 = 2
        S.dma("pool", qTs[:], qTb[:, 0:4, :], writes=[t_qTs])
        gate_dma(0, 0)
        gate_silu(0, 0)
        for n in range(LA2):
            sw_qk(sw_items[n])
        done_q = {0}
        for n, it in enumerate(sw_items):
            hg, i, pi, kb, which, np_ = it
            if pi == 0 and i % 4 == 0:
                nci = i // 4 + 1
                if nci < 8:
                    gate_dma(hg, nci)
                elif hg == 0:
                    gate_dma(1, 0)
            if pi == 0 and i % 4 == 2:
                nci = i // 4 + 1
                if nci < 8:
                    gate_silu(hg, nci)
                elif hg == 0:
                    gate_silu(1, 0)
            if n + LA2 < len(sw_items):
                nx = sw_items[n + LA2]
                if nx[0] not in done_q:
                    done_q.add(nx[0])
                    S.dma("pool", qTs[:], qTb[:, nx[0] * 4:(nx[0] + 1) * 4, :], writes=[t_qTs])
                sw_qk(nx)
            sw_rest(it)
        S.emit()
    return nc

import contextlib

SQ = 4096
NB = SQ // 128
NS = 8
NH = 32
NQ = NS * 128
SCALE = float(128 ** -0.5)
NEG_SEL = -3.0e38


def build_O(debug=False):
    nc = bass.Bass("TRN2", target_bir_lowering=False)
    st = contextlib.ExitStack()

    def din(name, shape):
        return nc.dram_tensor(name, shape, F32, kind="ExternalInput").ap()

    def dout(name, shape):
        return nc.dram_tensor(name, shape, F32, kind="ExternalOutput").ap()

    def sb(name, shape, dt):
        return st.enter_context(nc.sbuf_tensor(name, shape, dt))

    def ps(name, shape, dt):
        return st.enter_context(nc.psum_tensor(name, shape, dt))

    ident = din("ident", [128, 128])
    qiT = din("qiT", [NS, 64, NH * 128])
    kiT = din("kiT", [64, SQ])
    wi = din("wi", [128, NS * NH])
    cb = din("cb", [NS, 128, 512])
    QT = din("QT", [NH, 128, NQ])
    KT = din("KT", [NH, 128, SQ])
    V = din("V", [NH, 128, NB * 128])
    gT = din("gT", [NH, 128, NQ])
    rawb = din("rawb", [NH, 128, NS * 5 * 128])
    c31 = din("c31", [128, NH])
    yTo = dout("yTo", [NH, 128, NQ])
    dbg = dout("dbg", [NS, 128, SQ]) if debug else None
    dbg2 = dout("dbg2", [NS, 128, SQ]) if debug else None

    with st:
        S = Sched(nc)
        idf = sb("idf", [128, 128], F32)
        t_id = Tok()
        S.dma("sp", idf[:], ident, writes=[t_id])
        ones_bf = sb("ones_bf", [128, 128], BF16)
        t_ones = Tok()
        S.op("dve", lambda e: e.memset(ones_bf[:], 1.0), writes=[t_ones])
        c31t = sb("c31t", [128, NH], F32)
        t_c31 = Tok()
        S.dma("sp", c31t[:], c31, writes=[t_c31])
        nc31t = sb("nc31t", [128, NH], F32)
        S.op("dve", lambda e: e.tensor_scalar(out=nc31t[:], in0=c31t[:], scalar1=-1.0, scalar2=None, op0=ALU.mult), reads=[t_c31], writes=[t_c31])
        wit = sb("wit", [128, NS * NH], F32)
        t_wi = Tok()
        S.dma("sp", wit[:], wi, writes=[t_wi])
        maskT = sb("maskT", [128, NB, NQ], BF16)
        t_mT = [Tok() for _ in range(NS)]
        pS = [ps("pS%d" % i, [128, 512], F32) for i in range(4)]
        t_pS = [Tok() for _ in range(4)]
        pO = ps("pO", [128, 1024], F32)
        t_pO = Tok()
        pD = ps("pD", [128, 1024], F32)
        t_pD = Tok()
        kit = sb("kit", [64, SQ], BF16)
        t_kit = Tok()
        S.dma("pool", kit[:], kiT, writes=[t_kit])
        qit = [sb("qit%d" % i, [64, NH * 128], BF16) for i in range(2)]
        t_qit = [Tok(), Tok()]
        acc = sb("acc", [128, SQ], F32)
        t_acc = Tok()
        rl = [sb("rl%d" % i, [128, 512], F32) for i in range(2)]
        t_rl = [Tok(), Tok()]
        cbt = [sb("cbt%d" % i, [128, 512], F32) for i in range(2)]
        t_cbt = [Tok(), Tok()]
        mx = sb("mx", [128, 8], F32)
        t_mx = Tok()
        c1 = {"d": 0}
        for i in range(NS):
            n_i = 512 * (i + 1)
            qb = i % 2
            S.dma("pool", qit[qb][:], qiT[i], writes=[t_qit[qb]])
            S.dma("sp", cbt[qb][:], cb[i], writes=[t_cbt[qb]])
            for ch in range(i + 1):
                for h in range(NH):
                    db = c1["d"] % 2
                    c1["d"] += 1
                    S.op("pe", lambda e, db=db, h=h, ch=ch, qb=qb: e.matmul(pS[db][:, 0:512], lhsT=qit[qb][:, h * 128:(h + 1) * 128],
                                                                          rhs=kit[:, ch * 512:(ch + 1) * 512], start=True, stop=True),
                         reads=[t_qit[qb], t_kit], writes=[t_pS[db]])
                    S.op("act", lambda e, db=db: e.activation(out=rl[db][:], in_=pS[db][:, 0:512], func=AF.Relu), reads=[t_pS[db]], writes=[t_rl[db]])
                    wcol = wit[:, i * NH + h:i * NH + h + 1]
                    if h == 0:
                        S.op("dve", lambda e, db=db, ch=ch, wcol=wcol: e.tensor_scalar(out=acc[:, ch * 512:(ch + 1) * 512], in0=rl[db][:], scalar1=wcol,
                                                                                      scalar2=None, op0=ALU.mult),
                             reads=[t_rl[db], t_wi], writes=[t_acc])
                    else:
                        S.op("dve", lambda e, db=db, ch=ch, wcol=wcol: e.scalar_tensor_tensor(out=acc[:, ch * 512:(ch + 1) * 512], in0=rl[db][:], scalar=wcol,
                                                                                             in1=acc[:, ch * 512:(ch + 1) * 512], op0=ALU.mult, op1=ALU.add),
                             reads=[t_rl[db], t_wi, t_acc], writes=[t_acc])
            S.op("dve", lambda e, i=i, qb=qb: e.tensor_tensor(out=acc[:, i * 512:(i + 1) * 512], in0=acc[:, i * 512:(i + 1) * 512], in1=cbt[qb][:], op=ALU.add),
                 reads=[t_acc, t_cbt[qb]], writes=[t_acc])
            if debug:
                S.dma("sp", dbg2[i, :, 0:n_i], acc[:, 0:n_i], reads=[t_acc], final=True)
            for r in range(32):
                S.op("dve", lambda e, n_i=n_i: e.max(out=mx[:], in_=acc[:, 0:n_i]), reads=[t_acc], writes=[t_mx])
                S.op("dve", lambda e, n_i=n_i: e.match_replace(out=acc[:, 0:n_i], in_to_replace=mx[:], in_values=acc[:, 0:n_i], imm_value=NEG_SEL),
                     reads=[t_acc, t_mx], writes=[t_acc])
            S.op("dve", lambda e, n_i=n_i: e.tensor_scalar(out=acc[:, 0:n_i], in0=acc[:, 0:n_i], scalar1=-2.0e38, scalar2=None, op0=ALU.is_le),
                 reads=[t_acc], writes=[t_acc])
            if debug:
                S.dma("sp", dbg[i, :, 0:n_i], acc[:, 0:n_i], reads=[t_acc], final=True)
            for j0 in range(0, 4 * (i + 1), 8):
                def ftr(eng, j0=j0):
                    for jj in range(8):
                        ins = eng.transpose(pO[:, jj * 128:(jj + 1) * 128], acc[:, (j0 + jj) * 128:(j0 + jj + 1) * 128], idf[:])
                    return ins
                nj = min(8, 4 * (i + 1) - j0)

                def ftr2(eng, j0=j0, nj=nj):
                    for jj in range(nj):
                        ins = eng.transpose(pO[:, jj * 128:(jj + 1) * 128], acc[:, (j0 + jj) * 128:(j0 + jj + 1) * 128], idf[:])
                    return ins
                S.op("pe", ftr2, reads=[t_acc, t_id], writes=[t_pO], same_ok=True)
                S.op("act", lambda e, j0=j0, nj=nj, i=i: e.activation(out=maskT[:, j0:j0 + nj, i * 128:(i + 1) * 128],
                                                                     in_=pO[:, 0:nj * 128].rearrange("p (j q) -> p j q", j=nj), func=AF.Copy),
                     reads=[t_pO], writes=[t_mT[i]])
        kT = [sb("kT%d" % i, [128, SQ], BF16) for i in range(2)]
        vt = [sb("vt%d" % i, [128, NB, 128], BF16) for i in range(2)]
        qT = [sb("qT%d" % i, [128, NQ], BF16) for i in range(2)]
        eb = [sb("eb%d" % i, [128, NS * 5 * 128], BF16) for i in range(2)]
        gt = [sb("gt%d" % i, [128, NQ], F32) for i in range(2)]
        t_kT = [Tok(), Tok()]
        t_vt = [Tok(), Tok()]
        t_qT = [Tok(), Tok()]
        t_eb = [Tok(), Tok()]
        t_gt = [Tok(), Tok()]
        PT = [sb("PT%d" % i, [128, 512], BF16) for i in range(5)]
        t_PT = [Tok() for _ in range(5)]
        t_PTc = [Tok() for _ in range(5)]
        rd = sb("rd", [128, NQ], F32)
        t_rd = Tok()
        yt = sb("yt", [128, NQ], F32)
        t_yt = Tok()

        def head_setup(h):
            hb = h % 2
            S.dma("pool", kT[hb][:], KT[h], writes=[t_kT[hb]])
            S.dma("pool", vt[hb][:].rearrange("p j d -> p (j d)"), V[h], writes=[t_vt[hb]])
            S.dma("pool", qT[hb][:], QT[h], writes=[t_qT[hb]])
            S.dma("pool", eb[hb][:], rawb[h], writes=[t_eb[hb]])
            S.dma("sp", gt[hb][:], gT[h], writes=[t_gt[hb]])

        def head_compute(h):
            hb = h % 2
            S.op("act", lambda e: e.activation(out=eb[hb][:], in_=eb[hb][:], func=AF.Exp, bias=nc31t[:, h:h + 1]), reads=[t_eb[hb], t_c31], writes=[t_eb[hb]])
            for i in range(NS):
                jj0 = 1 if i == 0 else 0
                j0 = 4 * i - 1 + jj0
                nj = 5 - jj0

                def fcm(e, i=i, jj0=jj0, j0=j0, nj=nj):
                    ebv = eb[hb][:, (i * 5 + jj0) * 128:(i * 5 + 5) * 128].rearrange("p (j q) -> p j q", j=nj)
                    return e.tensor_tensor(out=ebv, in0=ebv, in1=maskT[:, j0:j0 + nj, i * 128:(i + 1) * 128], op=ALU.mult)
                S.op("dve", fcm, reads=[t_eb[hb], t_mT[i]], writes=[t_eb[hb]])
            S.op("act", lambda e: e.activation(out=gt[hb][:], in_=gt[hb][:], func=AF.Silu), reads=[t_gt[hb]], writes=[t_gt[hb]])

        c2 = {"s": 0, "pt": 0}
        state = {}

        def chunks(c0):
            return [(c0, 512), (512, NQ)] if c0 < 512 else [(c0, NQ)]

        def imin(j):
            return max(0, (j - 3 + 3) // 4)

        def emit_qk(it):
            h, j, A, B = it
            hb = h % 2
            sbuf = c2["s"] % 4
            c2["s"] += 1
            state[it] = sbuf
            S.op("pe", lambda e: e.matmul(pS[sbuf][:, 0:B - A], lhsT=kT[hb][:, j * 128:(j + 1) * 128], rhs=qT[hb][:, A:B], start=True, stop=True),
                 reads=[t_kT[hb], t_qT[hb]], writes=[t_pS[sbuf]])

        def emit_rest(it):
            h, j, A, B = it
            hb = h % 2
            i0 = imin(j)
            ilast_c = min(NS - 1, (j + 1) // 4)
            cend = (ilast_c + 1) * 128
            sbuf = state.pop(it)
            pk = c2["pt"] % 5
            c2["pt"] += 1
            W = B - A
            S.op("act", lambda e: e.activation(out=PT[pk][:, 0:W], in_=pS[sbuf][:, 0:W], func=AF.Exp, scale=SCALE, bias=c31t[:, h:h + 1]),
                 reads=[t_pS[sbuf], t_c31], writes=[t_PT[pk], t_PTc[pk]])
            f0 = max(cend, A)
            if f0 < B:
                S.op("dve", lambda e: e.tensor_tensor(out=PT[pk][:, f0 - A:W], in0=PT[pk][:, f0 - A:W], in1=maskT[:, j, f0:B], op=ALU.mult),
                     reads=[t_PT[pk]] + t_mT, writes=[t_PT[pk]])
            for i in range(i0, ilast_c + 1):
                if not (A <= i * 128 < B):
                    continue
                jj = j - (4 * i - 1)
                S.op("dve", lambda e, i=i, jj=jj: e.tensor_tensor(out=PT[pk][:, i * 128 - A:(i + 1) * 128 - A], in0=PT[pk][:, i * 128 - A:(i + 1) * 128 - A],
                                                                 in1=eb[hb][:, (i * 5 + jj) * 128:(i * 5 + jj + 1) * 128], op=ALU.mult),
                     reads=[t_PTc[pk], t_eb[hb]], writes=[t_PTc[pk]])
            lastj = (NB - 1) if A >= 512 else 15

            def f(eng):
                eng.matmul(pO[:, A:B], lhsT=vt[hb][:, j, :], rhs=PT[pk][:, 0:W], start=(j == 0), stop=(j == lastj))
                return eng.matmul(pD[:, A:B], lhsT=ones_bf[:], rhs=PT[pk][:, 0:W], start=(j == 0), stop=(j == lastj))
            S.op("pe", f, reads=[t_PT[pk], t_PTc[pk], t_vt[hb], t_ones], writes=[t_pO, t_pD], same_ok=True)
            if j == NB - 1:
                S.op("dve", lambda e: e.reciprocal(out=rd[:], in_=pD[:]), reads=[t_pD], writes=[t_rd])
                S.op("dve", lambda e: e.tensor_tensor(out=rd[:], in0=pO[:], in1=rd[:], op=ALU.mult), reads=[t_pO, t_rd], writes=[t_rd])
                S.op("dve", lambda e: e.tensor_tensor(out=yt[:], in0=rd[:], in1=gt[hb][:], op=ALU.mult), reads=[t_rd, t_gt[hb]], writes=[t_yt])
                S.dma("sp", yTo[h], yt[:], reads=[t_yt], final=True)

        items = []
        for h in range(NH):
            for j in range(NB):
                c0 = imin(j) * 128
                if c0 < 512:
                    items.append((h, j, c0, 512))
                items.append((h, j, max(c0, 512), NQ))
        LA = 3
        head_setup(0)
        head_compute(0)
        for n in range(LA):
            emit_qk(items[n])
        seen_h = set()
        seen_c = set()
        for n, it in enumerate(items):
            h, j, A, B = it
            if j == 0 and h + 1 < NH and h not in seen_h:
                seen_h.add(h)
                head_setup(h + 1)
            if j == 22 and h + 1 < NH and h not in seen_c:
                seen_c.add(h)
                head_compute(h + 1)
            if n + LA < len(items):
                emit_qk(items[n + LA])
            emit_rest(it)
        S.emit()
    return nc


import math as _math

NCORES = 8
SEQ = 4096
BATCH = 2
_PROGS = {}


def _prog(key, fn):
    if key not in _PROGS:
        _PROGS[key] = fn()
    return _PROGS[key]


def _run(nc, in_maps):
    res = run_bass_kernel_spmd(nc, in_maps, core_ids=list(range(NCORES)))
    return res.results


def _bc(v, n=128):
    v = np.asarray(v, dtype=np.float32).reshape(-1)
    return np.ascontiguousarray(np.broadcast_to(v[None, :], (n, v.shape[0])))


def _t5_bucket(rel_):
    n = np.maximum(rel_, 0)
    nf = np.maximum(n, 1).astype(np.float32)
    large = 16 + (np.log(nf / 16) / _math.log(128 / 16) * 16).astype(np.int32)
    large = np.minimum(large, 31)
    return np.where(n < 16, n, large)


def _slot_blocks(c):
    return [4 * i + (c if i % 2 == 0 else 3 - c) for i in range(8)]


_IDENT = np.eye(128, dtype=np.float32)


def _run_P(cfg, h, yT_full, w):
    nc = _prog(("P", cfg["outproj"], cfg["ple"], cfg["nxt"], cfg["final"]), lambda: build_P(cfg))
    maps = []
    for c in range(NCORES):
        b, qq = divmod(c, 4)
        rows = slice(c * TPC, (c + 1) * TPC)
        m = {"ident": _IDENT, "h_in": np.ascontiguousarray(h[rows])}
        if cfg["outproj"]:
            m["yT"] = np.ascontiguousarray(yT_full[b][:, qq * TPC:(qq + 1) * TPC])
            m["w_out"] = w["w_out"]
        if cfg["ple"]:
            m["p"] = np.ascontiguousarray(w["p"][rows])
            m["w_gate"] = w["w_gate"]
            m["w_pp"] = w["w_pp"]
            m["ple_g"] = w["ple_g"]
        if cfg["nxt"]:
            m["norm_g"] = w["norm_g"]
            m["w_in"] = w["w_in"]
        if cfg["nxt"] == "odd":
            for k in ("qn_g", "kvn_g", "w_uq", "w_uqi", "w_ukf", "w_uvf", "ln_g", "ln_b"):
                m[k] = w[k]
        if cfg["final"]:
            m["final_g"] = w["final_g"]
        maps.append(m)
    res = _run(nc, maps)
    out = {}
    for k in res[0].keys():
        out[k] = np.concatenate([res[c][k] for c in range(NCORES)], axis=0)
    return out


def _run_E(proj, b_f, sinks, t5_table):
    nc = _prog(("E",), build_E)
    rel_band = np.arange(128)[:, None] + 128 - np.arange(256)[None, :]
    valid = (rel_band >= 0) & (rel_band < 128)
    bb = t5_table[_t5_bucket(rel_band)][..., :32].transpose(2, 0, 1)
    bm = np.where(valid[None], bb, np.float32(-30000.0)).astype(np.float32)
    sel = np.zeros((4, 4, 128), np.float32)
    for hh in range(4):
        sel[hh, hh, :] = 1
    sel = sel.reshape(4, 512)
    id4 = np.eye(4, dtype=np.float32)
    triT = np.triu(np.ones((128, 128), np.float32))
    maps = []
    for c in range(NCORES):
        b, g = divmod(c, 4)
        pr = proj[b * SEQ:(b + 1) * SEQ]
        o = 0
        q_a = pr[:, 0:2048]; k_a = pr[:, 2048:4096]; v_a = pr[:, 4096:6144]; f_a = pr[:, 6144:6160]
        g_a = pr[:, 6160:8208]; q_b = pr[:, 8208:10256]; k_b = pr[:, 10256:10512]; v_b = pr[:, 10512:10768]; g_b = pr[:, 10768:12816]
        hs = slice(4 * g * 128, (4 * g + 4) * 128)

        def hT(a):
            return np.ascontiguousarray(a.reshape(SEQ, 4, 128).transpose(1, 2, 0))
        qs = slice(8 * g * 64, (8 * g + 8) * 64)

        def hT8(a):
            return np.ascontiguousarray(a.reshape(SEQ, 8, 64).transpose(2, 1, 0))
        bmg = bm[8 * g:8 * g + 8]
        bandT = np.stack([bmg[:, :, 0:128], bmg[:, :, 128:256]], axis=1)
        bandT = np.ascontiguousarray(bandT.transpose(3, 0, 1, 2)).reshape(128, 8 * 2 * 128)
        maps.append(dict(
            qTa=hT(q_a[:, hs]), kTa=hT(k_a[:, hs]), va=np.ascontiguousarray(v_a[:, hs].reshape(32, 128, 4, 128).transpose(2, 1, 0, 3)).reshape(4, 128, 32 * 128),
            gTa=hT(g_a[:, hs]), fT=np.ascontiguousarray(f_a[:, 4 * g:4 * g + 4].T), bf=np.ascontiguousarray(b_f[4 * g:4 * g + 4].reshape(4, 1)),
            sel=sel, id4=id4, triT=triT,
            qTb=hT8(q_b[:, qs]), kTb=np.ascontiguousarray(k_b[:, g * 64:(g + 1) * 64].T), vb=np.ascontiguousarray(v_b[:, g * 64:(g + 1) * 64].reshape(32, 128, 64).transpose(1, 0, 2)).reshape(128, 32 * 64),
            gTb=hT8(g_b[:, qs]), bandT=bandT, sinkb=_bc(sinks[8 * g:8 * g + 8], 64),
        ))
    res = _run(nc, maps)
    yT_full = [np.empty((4096, SEQ), np.float32) for _ in range(BATCH)]
    for c in range(NCORES):
        b, g = divmod(c, 4)
        yT_full[b][4 * g * 128:(4 * g + 4) * 128] = res[c]["yTa"].reshape(512, SEQ)
        yb = res[c]["yTb"]
        yT_full[b][2048 + 8 * g * 64:2048 + (8 * g + 8) * 64] = yb.transpose(1, 0, 2).reshape(512, SEQ)
    return yT_full


def _run_O(po, t5_table):
    nc = _prog(("O",), build_O)
    t5_c = np.ascontiguousarray(t5_table[:, 32:])
    NHh = 32
    maps = []
    qrows_all = []
    for c8 in range(NCORES):
        b, c = divmod(c8, 4)
        tok = slice(b * SEQ, (b + 1) * SEQ)
        q = po["o_q"][tok].reshape(SEQ, NHh, 128)
        k = po["o_k"][tok].reshape(SEQ, NHh, 128)
        v = po["o_v"][tok].reshape(SEQ, NHh, 128)
        g = po["proj"][tok][:, 1632:].reshape(SEQ, NHh, 128)
        q_idx = po["o_qi"][tok].reshape(SEQ, NHh, 64)
        k_idx = po["o_ki"][tok]
        w_idx = po["o_wi"][tok]
        blks = _slot_blocks(c)
        qrows = np.concatenate([np.arange(bk * 128, (bk + 1) * 128) for bk in blks])
        qrows_all.append(qrows)
        qi = q_idx[qrows].reshape(8, 128, NHh, 64)
        qiT = np.ascontiguousarray(qi.transpose(0, 3, 2, 1)).reshape(8, 64, NHh * 128)
        kiT = np.ascontiguousarray(k_idx.T)
        wi = np.ascontiguousarray(w_idx[qrows].reshape(8, 128, NHh).transpose(1, 0, 2)).reshape(128, 8 * NHh)
        cb = np.zeros((8, 128, 512), np.float32)
        rawb = np.empty((NHh, 128, 8, 5, 128), np.float32)
        for i, bk in enumerate(blks):
            qpos = bk * 128 + np.arange(128)
            kpos = i * 512 + np.arange(512)
            cb[i] = np.where(kpos[None, :] <= qpos[:, None], np.float32(0.0), np.float32(-1.0e30))
            for jj in range(5):
                j = 4 * i - 1 + jj
                if j < 0:
                    rawb[:, :, i, jj, :] = -30000.0
                    continue
                kp = j * 128 + np.arange(128)
                r = qpos[None, :] - kp[:, None]
                vals = t5_c[_t5_bucket(r)]
                vals = np.where((r >= 0)[:, :, None], vals, np.float32(-30000.0))
                rawb[:, :, i, jj, :] = vals.transpose(2, 0, 1)
        maps.append(dict(
            ident=_IDENT, qiT=qiT, kiT=kiT, wi=wi, cb=cb,
            QT=np.ascontiguousarray(q[qrows].transpose(1, 2, 0)), KT=np.ascontiguousarray(k.transpose(1, 2, 0)),
            V=np.ascontiguousarray(v.reshape(32, 128, NHh, 128).transpose(2, 1, 0, 3)).reshape(NHh, 128, 32 * 128),
            gT=np.ascontiguousarray(g[qrows].transpose(1, 2, 0)),
            rawb=rawb.reshape(NHh, 128, 8 * 5 * 128), c31=_bc(t5_c[31]),
        ))
    res = _run(nc, maps)
    yT_full = [np.empty((4096, SEQ), np.float32) for _ in range(BATCH)]
    for c8 in range(NCORES):
        b, c = divmod(c8, 4)
        yT_full[b][:, qrows_all[c8]] = res[c8]["yTo"].reshape(4096, 1024)
    return yT_full


def kernel(x, p, t5_table, norm_g, even_w_in, even_b_f, even_sinks, even_w_out,
           odd_w_in, odd_q_norm_g, odd_kv_norm_g, odd_w_uq, odd_w_uq_idx, odd_idx_ln_g,
           odd_idx_ln_b, odd_w_uk, odd_w_uv, odd_w_out, ple_w_proj, ple_norm_g,
           ple_w_gate, final_g):
    f32 = lambda a: np.ascontiguousarray(np.asarray(a, dtype=np.float32))
    x = f32(x); p = f32(p); t5_table = f32(t5_table)
    h = x.reshape(BATCH * SEQ, 4096)
    DEPTH = 4

    def nxt_weights(i):
        j = i // 2
        w = {"norm_g": _bc(norm_g[i])}
        if i % 2 == 0:
            w["w_in"] = f32(even_w_in[j])
        else:
            w["w_in"] = f32(odd_w_in[j])
            w["qn_g"] = _bc(odd_q_norm_g[j]); w["kvn_g"] = _bc(odd_kv_norm_g[j])
            w["w_uq"] = f32(odd_w_uq[j]); w["w_uqi"] = f32(odd_w_uq_idx[j])
            w["w_ukf"] = f32(np.asarray(odd_w_uk[j]).reshape(512, 4096)); w["w_uvf"] = f32(np.asarray(odd_w_uv[j]).reshape(512, 4096))
            w["ln_g"] = _bc(odd_idx_ln_g[j]); w["ln_b"] = _bc(odd_idx_ln_b[j])
        return w

    po = _run_P(dict(outproj=False, ple=False, nxt="even", final=False), h, None, nxt_weights(0))
    out = None
    for i in range(DEPTH):
        j = i // 2
        if i % 2 == 0:
            yT_full = _run_E(po["proj"], f32(even_b_f[j]), f32(even_sinks[j]), t5_table)
            w_out = f32(even_w_out[j])
        else:
            yT_full = _run_O(po, t5_table)
            w_out = f32(odd_w_out[j])
        w = {"w_out": w_out, "p": p[i].reshape(BATCH * SEQ, 256), "w_gate": f32(ple_w_gate[i]), "w_pp": f32(ple_w_proj[i]),
             "ple_g": _bc(ple_norm_g[i])}
        if i + 1 < DEPTH:
            w.update(nxt_weights(i + 1))
            cfg = dict(outproj=True, ple=True, nxt=("even" if (i + 1) % 2 == 0 else "odd"), final=False)
        else:
            w["final_g"] = _bc(final_g)
            cfg = dict(outproj=True, ple=True, nxt=None, final=True)
        po = _run_P(cfg, h, yT_full, w)
        h = po["h_out"]
        if cfg["final"]:
            out = po["out"]
    return out.reshape(BATCH, SEQ, 4096).astype(np.float32)
```
